# Optimizing a Trainium2 kernel written in Bass

```python
import jax, jax.numpy as jnp
from jax import lax
import numpy as np

D_MODEL = 2048
BATCH = 8
SEQ = 2048
DEPTH = 1

GRID_W = 64
CTX_LEN = 256
RET_HEADS = 8
RET_DK = 128
RET_DV = 256
RET_CHUNK = 128
RET_QK_W = RET_HEADS * RET_DK
RET_V_W = RET_HEADS * RET_DV
MLA_HEADS = 16
MLA_Q_RANK = 512
MLA_KV_RANK = 512
MLA_D_NOPE = 128
MLA_D_ROPE = 64
MLA_D_V = 128
MLA_V_W = MLA_HEADS * MLA_D_V
Q_BLOCK = 128
FFN_DIM = 5632
CONV_W = 3
ROPE_BASE = 10000.0
EPS = 1e-6
IN_SIZES = (RET_QK_W, RET_V_W, MLA_KV_RANK, MLA_D_ROPE,
            RET_QK_W, RET_V_W, MLA_Q_RANK, D_MODEL, D_MODEL)
CTX_COLS = RET_QK_W + RET_V_W + MLA_KV_RANK + MLA_D_ROPE
IN_COLS = CTX_COLS + RET_QK_W + RET_V_W + MLA_Q_RANK + 2 * D_MODEL

kernel_name = "hybrid_retention_mla_convffn_dit"


def rms_norm(t):
    tf = t.astype(jnp.float32)
    return (tf * lax.rsqrt(jnp.mean(tf * tf, axis=-1, keepdims=True) + EPS)).astype(t.dtype)


def split_cols(t, sizes):
    out, start = [], 0
    for s in sizes:
        out.append(t[..., start:start + s])
        start += s
    return out


def axial_rope(t, row, col):
    dr = t.shape[-1]
    q4 = dr // 4
    inv = ROPE_BASE ** (-jnp.arange(q4, dtype=jnp.float32) / q4)
    ang = jnp.stack([row[:, None] * inv, col[:, None] * inv], axis=1)
    ang = ang.reshape((t.shape[1],) + (1,) * (t.ndim - 3) + (2, q4))
    cos, sin = jnp.cos(ang).astype(t.dtype), jnp.sin(ang).astype(t.dtype)
    tr = t.reshape(t.shape[:-1] + (2, 2, q4))
    t1, t2 = tr[..., 0, :], tr[..., 1, :]
    return jnp.stack([t1 * cos - t2 * sin, t2 * cos + t1 * sin], axis=-2).reshape(t.shape)


def retention_scan(q, k, v, r0, log_gamma):
    b, L, h, _ = q.shape
    dv = v.shape[-1]
    n = L // RET_CHUNK
    pos = jnp.arange(RET_CHUNK, dtype=jnp.float32)
    diff = pos[:, None] - pos[None, :]
    lower = diff >= 0
    inner = jnp.where(lower[None], jnp.exp(jnp.where(lower, diff, 0.0)[None] * log_gamma[:, None, None]), 0.0)
    q_dec = jnp.exp((pos + 1.0)[:, None] * log_gamma[None, :])
    k_dec = jnp.exp((RET_CHUNK - 1.0 - pos)[:, None] * log_gamma[None, :])
    chunk_dec = jnp.exp(RET_CHUNK * log_gamma)

    def to_chunks(t):
        return t.reshape(b, n, RET_CHUNK, h, t.shape[-1]).transpose(1, 0, 2, 3, 4)

    def step(r, qkv):
        qc, kc, vc = qkv
        s = jnp.einsum('bihd,bjhd->bhij', qc, kc) * inner
        o = jnp.einsum('bhij,bjhe->bihe', s, vc) + jnp.einsum('bihd,bhde->bihe', qc * q_dec[:, :, None], r)
        r = r * chunk_dec[:, None, None] + jnp.einsum('bjhd,bjhe->bhde', kc * k_dec[:, :, None], vc)
        return r, o

    _, o = lax.scan(step, r0, (to_chunks(q), to_chunks(k), to_chunks(v)))
    return o.transpose(1, 0, 2, 3, 4).reshape(b, L, h, dv)


def mla_attention(q_nope, q_rope, k_nope, k_rope, v):
    b, s, h, dn = q_nope.shape
    dr = q_rope.shape[-1]
    nb = s // Q_BLOCK
    scale = (dn + dr) ** -0.5

    def block(args):
        qn, qr = args
        sc = jnp.einsum('bqhd,bkhd->bhqk', qn, k_nope) + jnp.einsum('bqhr,bkr->bhqk', qr, k_rope)
        p = jax.nn.softmax(sc.astype(jnp.float32) * scale, axis=-1).astype(v.dtype)
        return jnp.einsum('bhqk,bkhe->bqhe', p, v)

    qn_b = q_nope.reshape(b, nb, Q_BLOCK, h, dn).transpose(1, 0, 2, 3, 4)
    qr_b = q_rope.reshape(b, nb, Q_BLOCK, h, dr).transpose(1, 0, 2, 3, 4)
    o = lax.map(block, (qn_b, qr_b))
    return o.transpose(1, 0, 2, 3, 4).reshape(b, s, h * v.shape[-1])


def dwconv_centred(t, w, bias):
    L = t.shape[1]
    p = CONV_W // 2
    tp = jnp.pad(t, ((0, 0), (p, p), (0, 0)))
    out = bias
    for i in range(CONV_W):
        out = out + tp[:, i:i + L] * w[i]
    return out


def setup_inputs(seed: int = 0) -> dict:
    key = jax.random.key(seed)
    ks = jax.random.split(key, 24)

    def nrm(k, shape, s):
        return jax.random.normal(k, shape, jnp.float32) * s

    gam = 1.0 - 2.0 ** (-5.0 - np.arange(RET_HEADS))
    logit = jnp.asarray(np.log(gam / (1.0 - gam)), dtype=jnp.float32)
    return {
        "x": nrm(ks[0], (BATCH, SEQ, D_MODEL), 1.0),
        "c": nrm(ks[1], (BATCH, D_MODEL), 1.0),
        "ctx": nrm(ks[2], (BATCH, CTX_LEN, D_MODEL), 1.0),
        "c_ctx": nrm(ks[3], (D_MODEL,), 1.0),
        "w_ada": nrm(ks[4], (DEPTH, D_MODEL, 6 * D_MODEL), 0.5 * D_MODEL ** -0.5),
        "b_ada": nrm(ks[5], (DEPTH, 6 * D_MODEL), 0.01),
        "w_in": nrm(ks[6], (DEPTH, D_MODEL, IN_COLS), D_MODEL ** -0.5),
        "ret_decay_fwd": logit[None] + nrm(ks[7], (DEPTH, RET_HEADS), 0.05),
        "ret_decay_bwd": logit[None] + nrm(ks[8], (DEPTH, RET_HEADS), 0.05),
        "ret_gn": 1.0 + nrm(ks[9], (DEPTH, RET_V_W), 0.02),
        "w_ret_o": nrm(ks[10], (DEPTH, RET_V_W, D_MODEL), RET_V_W ** -0.5),
        "mla_q_norm": 1.0 + nrm(ks[11], (DEPTH, MLA_Q_RANK), 0.02),
        "w_q_up": nrm(ks[12], (DEPTH, MLA_Q_RANK, MLA_HEADS * (MLA_D_NOPE + MLA_D_ROPE)), MLA_Q_RANK ** -0.5),
        "mla_kv_norm": 1.0 + nrm(ks[13], (DEPTH, MLA_KV_RANK), 0.02),
        "w_kv_up": nrm(ks[14], (DEPTH, MLA_KV_RANK, MLA_HEADS * (MLA_D_NOPE + MLA_D_V)), MLA_KV_RANK ** -0.5),
        "w_mla_o": nrm(ks[15], (DEPTH, MLA_V_W, D_MODEL), MLA_V_W ** -0.5),
        "w_out": nrm(ks[16], (DEPTH, D_MODEL, D_MODEL), D_MODEL ** -0.5),
        "ffn_w_up": nrm(ks[17], (DEPTH, D_MODEL, 2 * FFN_DIM), D_MODEL ** -0.5),
        "ffn_conv_w": nrm(ks[18], (DEPTH, CONV_W, FFN_DIM), CONV_W ** -0.5),
        "ffn_conv_b": nrm(ks[19], (DEPTH, FFN_DIM), 0.01),
        "ffn_w_down": nrm(ks[20], (DEPTH, FFN_DIM, D_MODEL), FFN_DIM ** -0.5),
        "final_norm": 1.0 + nrm(ks[21], (D_MODEL,), 0.02),
    }


def reference(x, c, ctx, c_ctx, w_ada, b_ada, w_in, ret_decay_fwd, ret_decay_bwd, ret_gn, w_ret_o,
              mla_q_norm, w_q_up, mla_kv_norm, w_kv_up, w_mla_o, w_out,
              ffn_w_up, ffn_conv_w, ffn_conv_b, ffn_w_down, final_norm):
    b, s, d = x.shape
    lc = ctx.shape[1]
    rows = s // GRID_W
    row = jnp.repeat(jnp.arange(rows, dtype=jnp.float32), GRID_W)
    col = jnp.tile(jnp.arange(GRID_W, dtype=jnp.float32), rows)
    pos_c = jnp.arange(lc, dtype=jnp.float32)

    for l in range(DEPTH):
        mod = jax.nn.silu(c) @ w_ada[l] + b_ada[l]
        sh1, sc1, g1, sh2, sc2, g2 = jnp.split(mod, 6, axis=-1)
        mod_c = jax.nn.silu(c_ctx) @ w_ada[l][:, :2 * d] + b_ada[l][:2 * d]
        sh1c, sc1c = mod_c[:d], mod_c[d:]

        h = rms_norm(x) * (1.0 + sc1[:, None]) + sh1[:, None]
        hc = rms_norm(ctx) * (1.0 + sc1c) + sh1c
        rk, rv, kvd, kr, rq, rg, qd, gate_ret, gate_mla = split_cols(h @ w_in[l], IN_SIZES)
        rkc, rvc, kvdc, krc = split_cols(hc @ w_in[l][:, :CTX_COLS], IN_SIZES[:4])

        lg_f = jax.nn.log_sigmoid(ret_decay_fwd[l].astype(jnp.float32))
        lg_b = jax.nn.log_sigmoid(ret_decay_bwd[l].astype(jnp.float32))
        q_r = axial_rope(rq.reshape(b, s, RET_HEADS, RET_DK), row, col).astype(jnp.float32)
        k_r = (axial_rope(rk.reshape(b, s, RET_HEADS, RET_DK), row, col) * RET_DK ** -0.5).astype(jnp.float32)
        v_r = rv.reshape(b, s, RET_HEADS, RET_DV).astype(jnp.float32)
        k_rc = (rkc.reshape(b, lc, RET_HEADS, RET_DK) * RET_DK ** -0.5).astype(jnp.float32)
        v_rc = rvc.reshape(b, lc, RET_HEADS, RET_DV).astype(jnp.float32)
        w_cf = jnp.exp((lc - 1.0 - pos_c)[:, None] * lg_f[None])
        w_cb = jnp.exp(pos_c[:, None] * lg_b[None])
        r_ctx_f = jnp.einsum('bjhd,bjhe->bhde', k_rc * w_cf[:, :, None], v_rc)
        r_ctx_b = jnp.einsum('bjhd,bjhe->bhde', k_rc * w_cb[:, :, None], v_rc)
        o_f = retention_scan(q_r, k_r, v_r, r_ctx_f, lg_f)
        o_b = retention_scan(q_r[:, ::-1], k_r[:, ::-1], v_r[:, ::-1], r_ctx_b, lg_b)[:, ::-1]
        o_r = o_f + o_b
        o_r = o_r * lax.rsqrt(jnp.mean(o_r * o_r, axis=-1, keepdims=True) + EPS)
        o_r = o_r.reshape(b, s, RET_V_W).astype(x.dtype) * ret_gn[l] * jax.nn.silu(rg)
        ret_branch = o_r @ w_ret_o[l]

        q_m = (rms_norm(qd) * mla_q_norm[l]) @ w_q_up[l]
        q_m = q_m.reshape(b, s, MLA_HEADS, MLA_D_NOPE + MLA_D_ROPE)
        q_nope = q_m[..., :MLA_D_NOPE]
        q_rope = axial_rope(q_m[..., MLA_D_NOPE:], row, col)
        kv = ((rms_norm(kvd) * mla_kv_norm[l]) @ w_kv_up[l]).reshape(b, s, MLA_HEADS, MLA_D_NOPE + MLA_D_V)
        kvc = ((rms_norm(kvdc) * mla_kv_norm[l]) @ w_kv_up[l]).reshape(b, lc, MLA_HEADS, MLA_D_NOPE + MLA_D_V)
        k_rope = axial_rope(kr, row, col)
        k_nope_all = jnp.concatenate([kvc[..., :MLA_D_NOPE], kv[..., :MLA_D_NOPE]], axis=1)
        v_all = jnp.concatenate([kvc[..., MLA_D_NOPE:], kv[..., MLA_D_NOPE:]], axis=1)
        k_rope_all = jnp.concatenate([krc, k_rope], axis=1)
        mla_branch = mla_attention(q_nope, q_rope, k_nope_all, k_rope_all, v_all) @ w_mla_o[l]

        mixed = jax.nn.sigmoid(gate_ret) * ret_branch + jax.nn.sigmoid(gate_mla) * mla_branch
        x = x + g1[:, None] * (mixed @ w_out[l])

        h2 = rms_norm(x) * (1.0 + sc2[:, None]) + sh2[:, None]
        up = h2 @ ffn_w_up[l]
        a, v_f = up[..., :FFN_DIM], up[..., FFN_DIM:]
        a = dwconv_centred(a, ffn_conv_w[l], ffn_conv_b[l])
        x = x + g2[:, None] * ((jax.nn.silu(a) * v_f) @ ffn_w_down[l])

    return rms_norm(x) * final_norm
```

```python
import contextlib
import numpy as np
import concourse.bass as bass
import concourse.mybir as mybir
from concourse.bass_utils import run_bass_kernel_spmd

F32 = mybir.dt.float32
BF16 = mybir.dt.bfloat16
AF = mybir.ActivationFunctionType
ALU = mybir.AluOpType

COMPUTE = ("pe", "act", "dve", "pool")
QUEUES = ("sp", "pool")
NRING = 24

D = 2048
S = 2048
LC = 256
T = S + LC
NT = T // 128
FFN = 5632
NFC = FFN // 128
EPS = 1e-6
IN_COLS = 11328
C_RK, C_RV, C_KVD, C_KR, C_RQ, C_RG, C_QD, C_GR, C_GM = 0, 1024, 3072, 3584, 3648, 4672, 6720, 7232, 9280


class Op:
    __slots__ = ("eng", "fn", "deps", "signal", "value", "sem", "is_dma", "dma_k")

    def __init__(self, eng, fn, is_dma):
        self.eng = eng
        self.fn = fn
        self.deps = []
        self.signal = False
        self.value = None
        self.sem = None
        self.is_dma = is_dma
        self.dma_k = None


class Rec:
    def __init__(self):
        self.ops = {e: [] for e in ("pe", "act", "dve", "sp", "pool")}
        self.last_w = {}
        self.readers = {}
        self.dma_ops = {q: [] for q in QUEUES}
        self.last_real = {e: None for e in COMPUTE}
        self.bar_dma_start = {q: 0 for q in QUEUES}

    def _add(self, eng, fn, reads, writes, is_dma):
        op = Op(eng, fn, is_dma)
        deps = {}
        for k in reads:
            w = self.last_w.get(k)
            if w is not None:
                deps[id(w)] = (w, "raw")
        for k in writes:
            w = self.last_w.get(k)
            if w is not None:
                deps[id(w)] = (w, "waw")
            for r in self.readers.get(k, ()):
                if id(r) not in deps:
                    deps[id(r)] = (r, "war")
        for d, kind in deps.values():
            if d.eng == eng and not d.is_dma and not is_dma:
                if eng == "pe":
                    continue
            d.signal = True
            op.deps.append(d)
        for k in reads:
            self.readers.setdefault(k, []).append(op)
        for k in writes:
            self.last_w[k] = op
            self.readers[k] = []
        if is_dma:
            k = len(self.dma_ops[eng])
            op.dma_k = k
            op.signal = True
            if k >= NRING:
                op.deps.append(self.dma_ops[eng][k - NRING])
            self.dma_ops[eng].append(op)
        else:
            self.last_real[eng] = op
        self.ops[eng].append(op)
        return op

    def pe(self, fn, reads=(), writes=()):
        return self._add("pe", fn, reads, writes, False)

    def act(self, fn, reads=(), writes=()):
        return self._add("act", fn, reads, writes, False)

    def dve(self, fn, reads=(), writes=()):
        return self._add("dve", fn, reads, writes, False)

    def dma(self, q, fn, reads=(), writes=()):
        return self._add(q, fn, reads, writes, True)

    def pool(self, fn, reads=(), writes=()):
        return self._add("pool", fn, reads, writes, False)

    def barrier(self):
        deps = []
        for e in COMPUTE:
            if self.last_real[e] is not None:
                self.last_real[e].signal = True
                deps.append(self.last_real[e])
        for q in QUEUES:
            deps.extend(self.dma_ops[q][max(self.bar_dma_start[q], len(self.dma_ops[q]) - NRING):])
            self.bar_dma_start[q] = len(self.dma_ops[q])
        for e in ("pe", "act", "dve", "sp", "pool"):
            op = Op(e, None, False)
            op.deps = list(deps)
            self.ops[e].append(op)
        self.last_w = {}
        self.readers = {}

    def emit(self, nc, es, final_waits=()):
        n_sems = len(COMPUTE) + NRING * len(QUEUES)
        sems = [es.enter_context(nc.semaphore(f"s{i}")) for i in range(n_sems)]
        esem = {e: sems[i] for i, e in enumerate(COMPUTE)}
        qsem = {q: sems[len(COMPUTE) + qi * NRING: len(COMPUTE) + (qi + 1) * NRING]
                for qi, q in enumerate(QUEUES)}
        for e in COMPUTE:
            c = 0
            for op in self.ops[e]:
                if op.signal and not op.is_dma:
                    c += 1
                    op.sem = esem[e]
                    op.value = c
        for q in QUEUES:
            for op in self.ops[q]:
                if op.is_dma:
                    op.sem = qsem[q][op.dma_k % NRING]
                    op.value = 16 * (op.dma_k // NRING + 1)
        block = es.enter_context(nc.Block())

        def run(engname):
            def body(eng):
                waited = {}
                for op in self.ops[engname]:
                    for d in op.deps:
                        key = id(d.sem)
                        if waited.get(key, 0) >= d.value:
                            continue
                        waited[key] = d.value
                        eng.wait_ge(d.sem, d.value)
                    if op.fn is None:
                        continue
                    ins = op.fn(eng)
                    if op.signal:
                        ins.then_inc(op.sem, 16 if op.is_dma else 1)
                if engname == "sp":
                    for d in final_waits:
                        eng.wait_ge(d.sem, d.value)
            return body

        block.tensor(run("pe"))
        block.scalar(run("act"))
        block.vector(run("dve"))
        block.gpsimd(run("pool"))
        block.sync(run("sp"))


class Arena:
    def __init__(self, ap_all, nelem):
        self.ap = ap_all
        self.n = nelem
        self.off = 0

    def mark(self):
        return self.off

    def reset(self, m):
        self.off = m

    def alloc(self, shape, dt, parts=128):
        per = int(np.prod(shape[1:]))
        ne = per * (2 if dt == F32 else 1)
        ne = (ne + 15) // 16 * 16
        assert self.off + ne <= self.n, ("SBUF arena overflow", self.off, ne, self.n)
        a = self.ap[0:shape[0], self.off:self.off + per * (2 if dt == F32 else 1)]
        self.off += ne
        if dt == F32:
            a = a.bitcast(F32)
        if len(shape) == 3:
            a = a.rearrange("p (a b) -> p a b", a=shape[1])
        elif len(shape) == 4:
            a = a.rearrange("p (a b c) -> p a b c", a=shape[1], b=shape[2])
        return a


class Ring:
    def __init__(self, name, items):
        self.name = name
        self.items = items
        self.i = 0

    def next(self):
        j = self.i % len(self.items)
        self.i += 1
        return self.items[j], (self.name, j)


def _rope_tables(dr, nrep):
    q4 = dr // 4
    rows = S // 64
    row = np.repeat(np.arange(rows, dtype=np.float32), 64)
    col = np.tile(np.arange(64, dtype=np.float32), rows)
    inv = (np.float32(10000.0) ** (-np.arange(q4, dtype=np.float32) / np.float32(q4))).astype(np.float32)
    cos = np.zeros((dr, S), np.float32)
    sin = np.zeros((dr, S), np.float32)
    for d in range(dr):
        a = d // (2 * q4)
        b = (d % (2 * q4)) // q4
        m = d % q4
        ang = ((row if a == 0 else col) * inv[m]).astype(np.float32)
        cos[d] = np.cos(ang)
        sin[d] = np.sin(ang) * (-1.0 if b == 0 else 1.0)
    return np.stack([np.tile(cos, (nrep, 1)), np.tile(sin, (nrep, 1))]).astype(np.float32)


def _perm(q4):
    P = np.zeros((128, 128), np.float32)
    for m in range(128):
        k = m + q4 if (m % (2 * q4)) < q4 else m - q4
        P[k, m] = 1.0
    return P


def _consts():
    p = np.arange(128, dtype=np.float32)
    jj = p[:, None]
    ii = p[None, :]
    ctab = np.stack([
        np.broadcast_to(ii + 1.0, (128, 128)),
        np.broadcast_to(128.0 - ii, (128, 128)),
        np.maximum(ii - jj, 0.0),
        (ii >= jj).astype(np.float32),
        np.maximum(jj - ii, 0.0),
        (jj >= ii).astype(np.float32),
    ], axis=1).astype(np.float32)
    pcol = np.stack([127.0 - p, p], axis=1).astype(np.float32)
    return {
        "ident": np.eye(128, dtype=np.float32),
        "ctab": np.ascontiguousarray(ctab),
        "pcol": np.ascontiguousarray(pcol),
        "ropeR": _rope_tables(128, 1),
        "ropeM": _rope_tables(64, 2),
        "perm": np.stack([_perm(32), _perm(16)]).astype(np.float32),
    }


def build(dbg=False, upto="Z"):
    nc = bass.Bass("TRN2", target_bir_lowering=False)
    R = Rec()

    def din(name, shape, dt=F32):
        return nc.dram_tensor(name, list(shape), dt, kind="ExternalInput").ap()

    def dscr(name, shape, dt):
        return nc.dram_tensor(name, list(shape), dt, kind="ExternalOutput" if dbg else "Internal").ap()

    x = din("x", [S, D]); ctx = din("ctx", [LC, D]); cc = din("cc", [128, 16, 2])
    w_ada = din("w_ada", [D, 6 * D]); b_ada = din("b_ada", [6 * D])
    w_in = din("w_in", [D, IN_COLS])
    decay = din("decay", [16])
    ret_gn = din("ret_gn", [D])
    w_ret_o = din("w_ret_o", [D, D])
    q_norm = din("mla_q_norm", [512]); kv_norm = din("mla_kv_norm", [512])
    w_q_up = din("w_q_up", [512, 3072]); w_kv_up = din("w_kv_up", [512, 4096])
    w_mla_o = din("w_mla_o", [D, D]); w_out = din("w_out", [D, D])
    w_up = din("ffn_w_up", [D, 2 * FFN]); conv_w = din("conv_w", [128, 3, NFC]); conv_b = din("conv_b", [128, NFC])
    w_down = din("ffn_w_down", [FFN, D]); fnorm = din("final_norm", [D])
    ident_d = din("ident", [128, 128]); ctab_d = din("ctab", [128, 6, 128]); pcol_d = din("pcol", [128, 2])
    ropeR_d = din("ropeR", [2, 128, S]); ropeM_d = din("ropeM", [2, 128, S]); perm_d = din("perm", [2, 128, 128])
    out = nc.dram_tensor("out", [S, D], F32, kind="ExternalOutput").ap()

    modd = dscr("modd", [2, 6 * D], F32)
    kT_d = dscr("kT_d", [8, 128, T], BF16); qT_d = dscr("qT_d", [8, 128, S], BF16)
    v_d = dscr("v_d", [T, D], BF16); rgs_d = dscr("rgs_d", [S, D], BF16)
    gr_d = dscr("gr_d", [16, 128, S], BF16); gm_d = dscr("gm_d", [16, 128, S], BF16)
    kvnT_d = dscr("kvnT_d", [4, 128, T], BF16); qnT_d = dscr("qnT_d", [4, 128, S], BF16)
    krT_d = dscr("krT_d", [64, T], BF16)
    ogT_d = dscr("ogT_d", [16, 128, S], BF16); omT_d = dscr("omT_d", [16, 128, S], BF16)
    mixT_d = dscr("mixT_d", [16, 128, S], BF16)
    x1_d = dscr("x1_d", [S, D], F32); x2_d = dscr("x2_d", [S, D], F32)

    es = contextlib.ExitStack()
    NAR = 100000
    arena_t = es.enter_context(nc.sbuf_tensor("arena", [128, NAR], BF16))
    AR = Arena(arena_t, NAR)
    ps = [es.enter_context(nc.psum_tensor(f"ps{i}", [128, 512], F32)) for i in range(8)]
    psb = [p[:].bitcast(BF16) for p in ps]

    def PK(i):
        return ("ps", i)

    def wload(ring, src, nk, ncols):
        buf, key = ring.next()
        dst = buf[:, 0:nk * ncols].rearrange("p (k n) -> p k n", k=nk)
        srcv = src.rearrange("(k p) n -> p k n", p=128)
        nsplit = 4
        bounds = [(i * nk) // nsplit for i in range(nsplit + 1)]
        keys = []
        for i in range(nsplit):
            a, b = bounds[i], bounds[i + 1]
            pk = (key, i)
            R.dma("pool", lambda e, a=a, b=b: e.dma_start(out=dst[:, a:b, :], in_=srcv[:, a:b, :]), writes=[pk])
            keys += [pk] * (b - a)
        return dst, keys, buf

    def mm(o, lhsT, rhs, start, stop, reads, wkey):
        R.pe(lambda e: e.matmul(o, lhsT=lhsT, rhs=rhs, start=start, stop=stop), reads=reads, writes=[wkey])

    def tr(o, in_, ident, reads, wkey):
        R.pe(lambda e: e.transpose(out=o, in_=in_, identity=ident), reads=reads, writes=[wkey])

    def act(o, in_, func, reads, writes, **kw):
        R.act(lambda e: e.activation(out=o, in_=in_, func=func, **kw), reads=reads, writes=writes)

    def rstd_from_ssq(ssq_col, tmp_col, rstd_col, n, key_ssq, key_rstd):
        act(tmp_col, ssq_col, AF.Sqrt, [key_ssq], [key_rstd], scale=1.0 / n, bias=eps_t[:, 0:1])
        R.dve(lambda e: e.reciprocal(out=rstd_col, in_=tmp_col), reads=[key_rstd], writes=[key_rstd])

    identf = AR.alloc([128, 128], F32)
    identb = AR.alloc([128, 128], BF16)
    eps_t = AR.alloc([128, 2], F32)
    R.dma("sp", lambda e: e.dma_start(out=identf, in_=ident_d), writes=["identf"])
    R.act(lambda e: e.copy(out=identb, in_=identf), reads=["identf"], writes=["identb"])
    R.dve(lambda e: e.memset(eps_t[:, 0:1], EPS), writes=["eps"])
    R.dve(lambda e: e.memset(eps_t[:, 1:2], 1.0), writes=["eps"])
    base_mark = AR.mark()
    final_ops = []

    def bcast_load(dst, src_row, key):
        R.dma("sp", lambda e: e.dma_start(out=dst, in_=src_row.partition_broadcast(128)), writes=[key])

    cct = AR.alloc([128, 16, 2], F32)
    sb16 = AR.alloc([128, 16, 2], BF16)
    bst = [AR.alloc([2, 512], F32) for _ in range(2)]
    mst = [AR.alloc([2, 512], F32) for _ in range(2)]
    bring = Ring("bst", bst)
    mring = Ring("mst", mst)
    hT = AR.alloc([128, 16, T], BF16)
    after_hT = AR.mark()
    wbufA = [AR.alloc([128, 8192], BF16) for _ in range(2)]
    wringA = Ring("wA", wbufA)
    R.dma("sp", lambda e: e.dma_start(out=cct, in_=cc), writes=["cct"])
    act(sb16, cct, AF.Silu, ["cct"], ["sb16"])
    psr0 = Ring("ps", [0, 1])

    def mod_group(n, wring_, psr_):
        wt, wkey, _ = wload(wring_, w_ada[:, n * 512:(n + 1) * 512], 16, 512)
        bt, bkey = bring.next()
        R.dma("sp", lambda e: e.dma_start(out=bt, in_=b_ada[n * 512:(n + 1) * 512].partition_broadcast(2)), writes=[bkey])
        bi, _ = psr_.next()
        for kc in range(16):
            mm(ps[bi][0:2, :], sb16[:, kc, :], wt[:, kc, :], kc == 0, kc == 15, ["sb16", wkey[kc]], PK(bi))
        mt, mkey = mring.next()
        R.dve(lambda e: e.tensor_tensor(out=mt, in0=ps[bi][0:2, :], in1=bt, op=ALU.add), reads=[bkey], writes=[mkey, PK(bi)])
        R.dma("sp", lambda e: e.dma_start(out=modd[:, n * 512:(n + 1) * 512], in_=mt), reads=[mkey], writes=[("modd", n)])

    for n in range(8):
        mod_group(n, wringA, psr0)

    xt = [AR.alloc([128, D], F32) for _ in range(2)]
    xring = Ring("xt", xt)
    tmpA = AR.alloc([128, D], F32)
    junkA = AR.alloc([128, D], BF16)
    hb = [AR.alloc([128, D], BF16) for _ in range(2)]
    hbring = Ring("hb", hb)
    scp = AR.alloc([128, D], F32)
    shb = AR.alloc([128, D], F32)
    ssqA = AR.alloc([128, NT], F32)
    rsA = AR.alloc([128, NT], F32)
    rstdA = AR.alloc([128, NT], F32)
    psrA = Ring("ps", [2, 3, 4, 5])
    bufsA = (xring, junkA, tmpA, hbring, psrA)

    def norm_mod_T(src_rows, dstT, ncols_tok, tile_idx, scp_t, shb_t, keys_mod, ssq, rs, rstd, dst_key, bufs=None):
        xring, junkA, tmpA, hbring, psrA = bufs if bufs is not None else bufsA
        xtt, xkey = xring.next()
        R.dma("sp", lambda e: e.dma_start(out=xtt, in_=src_rows), writes=[xkey])
        i = tile_idx
        act(junkA, xtt, AF.Square, [xkey], ["junkA", ("ssq", i)], accum_out=ssq[:, i:i + 1])
        act(rs[:, i:i + 1], ssq[:, i:i + 1], AF.Sqrt, [("ssq", i)], [("rstd", i)], scale=1.0 / D, bias=eps_t[:, 0:1])
        yield
        R.dve(lambda e: e.reciprocal(out=rstd[:, i:i + 1], in_=rs[:, i:i + 1]), reads=[("rstd", i)], writes=[("rstd", i)])
        R.dve(lambda e: e.scalar_tensor_tensor(out=tmpA, in0=xtt, scalar=rstd[:, i:i + 1], in1=scp_t, op0=ALU.mult, op1=ALU.mult),
              reads=[xkey, ("rstd", i)] + keys_mod, writes=["tmpA"])
        hbt, hkey = hbring.next()
        HS = D // 2
        R.dve(lambda e: e.tensor_tensor(out=hbt[:, 0:HS], in0=tmpA[:, 0:HS], in1=shb_t[:, 0:HS], op=ALU.add),
              reads=["tmpA"] + keys_mod, writes=[(hkey, 0)])
        R.dve(lambda e: e.tensor_tensor(out=hbt[:, HS:D], in0=tmpA[:, HS:D], in1=shb_t[:, HS:D], op=ALU.add),
              reads=["tmpA"] + keys_mod, writes=[(hkey, 1)])
        for half in range(2):
            bi, _ = psrA.next()
            for k8 in range(8):
                kc = half * 8 + k8
                tr(psb[bi][:, k8 * 128:(k8 + 1) * 128], hbt[:, kc * 128:(kc + 1) * 128], identb, [(hkey, half), "identb"], PK(bi))
            R.act(lambda e, bi=bi, half=half: e.copy(out=dstT[:, half * 8:half * 8 + 8, ncols_tok:ncols_tok + 128],
                                                     in_=psb[bi].rearrange("p (k n) -> p k n", k=8)),
                  reads=[], writes=[PK(bi), dst_key])

    def run_tiles(jobs):
        gens = [None] * len(jobs)

        def start(k):
            gens[k] = jobs[k][0]()
            next(gens[k])
        for k in range(min(2, len(jobs))):
            start(k)
        for k in range(len(jobs)):
            if jobs[k][1] is not None:
                jobs[k][1]()
            for _ in gens[k]:
                pass
            if k + 2 < len(jobs):
                start(k + 2)

    def modrow(r, j):
        return modd[r, j * D:(j + 1) * D]

    def modkeys(j):
        return [("modd", n) for n in range(4 * j, 4 * j + 4)]

    def mod_ctx():
        R.dma("sp", lambda e: e.dma_start(out=scp, in_=modrow(1, 1).partition_broadcast(128)), reads=modkeys(1), writes=["scp"])
        R.dma("sp", lambda e: e.dma_start(out=shb, in_=modrow(1, 0).partition_broadcast(128)), reads=modkeys(0), writes=["shb"])
        R.dve(lambda e: e.tensor_scalar_add(out=scp, in0=scp, scalar1=1.0), reads=["scp"], writes=["scp"])

    def mod_lat():
        R.dma("sp", lambda e: e.dma_start(out=scp, in_=modrow(0, 1).partition_broadcast(128)), reads=modkeys(1), writes=["scp"])
        R.dma("sp", lambda e: e.dma_start(out=shb, in_=modrow(0, 0).partition_broadcast(128)), reads=modkeys(0), writes=["shb"])
        R.dve(lambda e: e.tensor_scalar_add(out=scp, in0=scp, scalar1=1.0), reads=["scp"], writes=["scp"])

    jobsA = []
    for i in range(NT):
        src = ctx[i * 128:(i + 1) * 128, :] if i < 2 else x[(i - 2) * 128:(i - 1) * 128, :]
        jobsA.append((lambda i=i, src=src: norm_mod_T(src, hT, i * 128, i, scp, shb, ["scp", "shb"], ssqA, rsA, rstdA, ("hT", i)),
                      mod_ctx if i == 0 else (mod_lat if i == 2 else None)))
    run_tiles(jobsA)

    if dbg:
        hT_o = nc.dram_tensor("hT_o", [16, 128, T], BF16, kind="ExternalOutput").ap()
        final_ops.append(R.dma("sp", lambda e: e.dma_start(out=hT_o.rearrange("k p n -> p k n"), in_=hT),
                               reads=[("hT", i) for i in range(NT)]))

    def finish():
        R.barrier()
        if dbg:
            pass
        R.emit(nc, es, final_waits=final_ops + R.dma_ops["sp"][-NRING:] + R.dma_ops["pool"][-NRING:])
        es.close()
        return nc

    if upto == "A":
        return finish()

    R.barrier()
    AR.reset(after_hT)
    wbufB = [AR.alloc([128, 8192], BF16) for _ in range(3)]
    wring = Ring("wB", wbufB)
    permb = AR.alloc([128, 2, 128], BF16)
    R.dma("pool", lambda e: e.dma_start(out=permb, in_=perm_d.rearrange("t p n -> p t n")), writes=["permb"])
    rawb_ring = Ring("rawb", [AR.alloc([128, 512], BF16) for _ in range(3)])
    cosT = AR.alloc([128, S], F32)
    sinT = AR.alloc([128, S], F32)
    t1b = AR.alloc([128, 512], F32)
    t2b = AR.alloc([128, 512], F32)
    stk_ring = Ring("stk", [AR.alloc([128, T], BF16) for _ in range(2)])
    stv_ring = Ring("stv", [AR.alloc([128, 4, 512], BF16) for _ in range(2)])
    nrmb = AR.alloc([128, 512], F32)
    gnb = AR.alloc([128, 512], F32)
    tmprg_ring = Ring("tmprg", [AR.alloc([128, 512], F32) for _ in range(1)])
    ktm_ring = Ring("ktm", [AR.alloc([128, 512], BF16) for _ in range(2)])
    kst_ring = Ring("kst", [AR.alloc([128, 4, 128], BF16) for _ in range(2)])
    junkB = AR.alloc([128, 512], BF16)
    ssqB = AR.alloc([128, 2 * NT], F32)
    rsB = AR.alloc([128, 2 * NT], F32)
    rstdB = AR.alloc([128, 2 * NT], F32)
    psr = Ring("ps", list(range(8)))
    TG = [(0, 256)] + [(256 + 512 * g, 512) for g in range(4)]
    mod_next = [8]

    def more_mod(k=1):
        for _ in range(k):
            if mod_next[0] < 24:
                mod_group(mod_next[0], wring, psr)
                mod_next[0] += 1

    def hkeys(t0, n):
        return [("hT", i) for i in range(t0 // 128, (t0 + n) // 128)]

    def swap_copy(dst_flat, src_flat, n, q4, rkey, wkey):
        sv = src_flat[:, 0:n].rearrange("p (g b m) -> p g b m", b=2, m=q4)
        dv = dst_flat[:, 0:n].rearrange("p (g b m) -> p g b m", b=2, m=q4)
        R.dve(lambda e: e.tensor_copy(out=dv[:, :, 0, :], in_=sv[:, :, 1, :]), reads=[rkey], writes=[wkey])
        R.dve(lambda e: e.tensor_copy(out=dv[:, :, 1, :], in_=sv[:, :, 0, :]), reads=[rkey], writes=[wkey])

    def load_rope(tab_d, np_):
        R.dma("sp", lambda e: e.dma_start(out=cosT[0:np_, :], in_=tab_d[0, 0:np_, :]), writes=["rope"])
        R.dma("sp", lambda e: e.dma_start(out=sinT[0:np_, :], in_=tab_d[1, 0:np_, :]), writes=["rope"])

    def rope_group(wt_l, pidx, wkey, np_, scale, with_ctx, stage, skey):
        groups = [(t0, n) for (t0, n) in TG if not (t0 == 0 and not with_ctx)]
        st = {}

        def ga(k):
            t0, n = groups[k]
            bi, _ = psr.next()
            for kc in range(16):
                mm(ps[bi][0:np_, 0:n], wt_l[kc], hT[:, kc, t0:t0 + n], kc == 0, kc == 15, [wkey[kc]] + hkeys(t0, n), PK(bi))
            if t0 == 0:
                st[k] = (bi, None, None)
                return
            rawb, rkey = rawb_ring.next()
            act(rawb[0:np_, 0:n], ps[bi][0:np_, 0:n], AF.Copy, [], [PK(bi), rkey])
            st[k] = (bi, rawb, rkey)

        def gb(k):
            t0, n = groups[k]
            off = t0 if with_ctx else t0 - LC
            bi, rawb, rkey = st[k]
            if t0 == 0:
                act(stage[0:np_, off:off + n], ps[bi][0:np_, 0:n], AF.Copy, [], [PK(bi), skey], scale=scale)
                return
            bj, _ = psr.next()
            mm(ps[bj][0:np_, 0:n], permb[0:np_, pidx, 0:np_], rawb[0:np_, 0:n], True, True, ["permb", rkey], PK(bj))
            lt = t0 - LC
            R.dve(lambda e: e.scalar_tensor_tensor(out=t1b[0:np_, 0:n], in0=ps[bi][0:np_, 0:n], scalar=scale,
                                                   in1=cosT[0:np_, lt:lt + n], op0=ALU.mult, op1=ALU.mult),
                  reads=["rope"], writes=[PK(bi), "t1b"])
            R.dve(lambda e: e.scalar_tensor_tensor(out=t2b[0:np_, 0:n], in0=ps[bj][0:np_, 0:n], scalar=scale,
                                                   in1=sinT[0:np_, lt:lt + n], op0=ALU.mult, op1=ALU.mult),
                  reads=["rope"], writes=[PK(bj), "t2b"])
            R.dve(lambda e: e.tensor_tensor(out=stage[0:np_, off:off + n], in0=t1b[0:np_, 0:n], in1=t2b[0:np_, 0:n], op=ALU.add),
                  reads=["t1b", "t2b"], writes=[skey])

        ga(0)
        for k in range(len(groups)):
            if k + 1 < len(groups):
                ga(k + 1)
            gb(k)

    load_rope(ropeM_d, 64)
    wt64, wkey64, wflat64 = wload(wring, w_in[:, C_KR:C_KR + 64], 16, 64)
    stage, skey = stk_ring.next()
    rope_group([wt64[:, kc, :] for kc in range(16)], 1, wkey64, 64, 1.0, True, stage, skey)
    R.dma("sp", lambda e: e.dma_start(out=krT_d, in_=stage[0:64, :]), reads=[skey], writes=["krT_d"])

    load_rope(ropeR_d, 128)

    def proj_rope_fm(col0, ncg, scale, with_ctx, dst_d, dname):
        ntok = T if with_ctx else S
        for cg in range(ncg):
            wt, wkey, wflat = wload(wring, w_in[:, col0 + cg * 512:col0 + (cg + 1) * 512], 16, 512)
            for hh in range(4):
                head = cg * 4 + hh
                stage, skey = stk_ring.next()
                rope_group([wt[:, kc, hh * 128:(hh + 1) * 128] for kc in range(16)], 0, wkey, 128, scale, with_ctx, stage, skey)
                R.dma("sp", lambda e, head=head, stage=stage: e.dma_start(out=dst_d[head], in_=stage[:, 0:ntok]),
                      reads=[skey], writes=[(dname, head)])
            more_mod()

    proj_rope_fm(C_RK, 2, 128.0 ** -0.5, True, kT_d, "kT_d")
    proj_rope_fm(C_RQ, 2, 1.0, False, qT_d, "qT_d")

    def proj_tm(col0, ncg, tiles, evac, dst_d, row0, dname, pre=None):
        for cg in range(ncg):
            wt, wkey, _ = wload(wring, w_in[:, col0 + cg * 512:col0 + (cg + 1) * 512], 16, 512)
            if pre is not None:
                pre(cg)
            stv = svkey = first_t = None
            for idx, t in enumerate(tiles):
                j = idx % 4
                if j == 0:
                    stv, svkey = stv_ring.next()
                    first_t = t
                bi, _ = psr.next()
                for kc in range(16):
                    mm(ps[bi][:, :], hT[:, kc, t * 128:(t + 1) * 128], wt[:, kc, :], kc == 0, kc == 15, [wkey[kc], ("hT", t)], PK(bi))
                evac(stv[:, j, :], bi, svkey, cg)
                if j == 3 or idx == len(tiles) - 1:
                    nn = j + 1
                    r0 = (first_t - row0) * 128
                    R.dma("sp", lambda e, r0=r0, nn=nn, cg=cg, stv=stv: e.dma_start(
                        out=dst_d[r0:r0 + nn * 128, cg * 512:(cg + 1) * 512].rearrange("(t p) c -> p t c", p=128), in_=stv[:, 0:nn, :]),
                        reads=[svkey], writes=[(dname, cg, first_t)])
            more_mod()

    def evac_copy(dst, bi, svkey, cg):
        act(dst, ps[bi][:, :], AF.Copy, [], [PK(bi), svkey])

    proj_tm(C_RV, 4, list(range(NT)), evac_copy, v_d, 0, "v_d")

    def pre_gn(cg):
        bcast_load(gnb, ret_gn[cg * 512:(cg + 1) * 512], "gnb")

    def evac_rg(dst, bi, svkey, cg):
        tmp, tkey = tmprg_ring.next()
        act(tmp, ps[bi][:, :], AF.Silu, [], [PK(bi), tkey])
        R.dve(lambda e: e.tensor_tensor(out=dst, in0=tmp, in1=gnb, op=ALU.mult), reads=[tkey, "gnb"], writes=[svkey])

    proj_tm(C_RG, 4, list(range(2, NT)), evac_rg, rgs_d, 2, "rgs_d", pre=pre_gn)

    def proj_norm_T(col0, tiles, norm_row, dstT_d, tok_tile0, dname, sbase):
        wt, wkey, _ = wload(wring, w_in[:, col0:col0 + 512], 16, 512)
        bcast_load(nrmb, norm_row, "nrmb")
        banks = {}

        def pn_a(idx):
            t = tiles[idx]
            bi, _ = psr.next()
            banks[idx] = bi
            for kc in range(16):
                mm(ps[bi][:, :], hT[:, kc, t * 128:(t + 1) * 128], wt[:, kc, :], kc == 0, kc == 15, [wkey[kc], ("hT", t)], PK(bi))

        def pn_b(idx):
            t = tiles[idx]
            si = sbase + idx
            bi = banks[idx]
            act(junkB, ps[bi][:, :], AF.Square, [], [PK(bi), "junkB", ("ssqB", si)], accum_out=ssqB[:, si:si + 1])
            rstd_from_ssq(ssqB[:, si:si + 1], rsB[:, si:si + 1], rstdB[:, si:si + 1], 512, ("ssqB", si), ("rstdB", si))
            ktm, kkey = ktm_ring.next()
            R.dve(lambda e: e.scalar_tensor_tensor(out=ktm, in0=ps[bi][:, :], scalar=rstdB[:, si:si + 1], in1=nrmb,
                                                   op0=ALU.mult, op1=ALU.mult),
                  reads=[("rstdB", si), "nrmb"], writes=[PK(bi), kkey])
            bj, _ = psr.next()
            for k4 in range(4):
                tr(psb[bj][:, k4 * 128:(k4 + 1) * 128], ktm[:, k4 * 128:(k4 + 1) * 128], identb, [kkey, "identb"], PK(bj))
            kst, kskey = kst_ring.next()
            R.act(lambda e: e.copy(out=kst, in_=psb[bj][:, 0:512].rearrange("p (k n) -> p k n", k=4)),
                  reads=[], writes=[PK(bj), kskey])
            c0 = (t - tok_tile0) * 128
            R.dma("sp", lambda e: e.dma_start(out=dstT_d[:, :, c0:c0 + 128].rearrange("k p n -> p k n"), in_=kst),
                  reads=[kskey], writes=[(dname, t)])

        pn_a(0)
        for idx in range(len(tiles)):
            if idx + 1 < len(tiles):
                pn_a(idx + 1)
            pn_b(idx)
        more_mod()

    proj_norm_T(C_KVD, list(range(NT)), kv_norm, kvnT_d, 0, "kvnT_d", 0)
    proj_norm_T(C_QD, list(range(2, NT)), q_norm, qnT_d, 2, "qnT_d", NT)

    for (col0, dst_d, dname) in ((C_GR, gr_d, "gr_d"), (C_GM, gm_d, "gm_d")):
        for cg in range(4):
            wt, wkey, _ = wload(wring, w_in[:, col0 + cg * 512:col0 + (cg + 1) * 512], 16, 512)
            for c4 in range(4):
                stage, skey = stk_ring.next()
                for g in range(4):
                    t0 = LC + 512 * g
                    bi, _ = psr.next()
                    for kc in range(16):
                        mm(ps[bi][:, :], wt[:, kc, c4 * 128:(c4 + 1) * 128], hT[:, kc, t0:t0 + 512], kc == 0, kc == 15,
                           [wkey[kc]] + hkeys(t0, 512), PK(bi))
                    act(stage[:, g * 512:(g + 1) * 512], ps[bi][:, :], AF.Sigmoid, [], [PK(bi), skey])
                R.dma("sp", lambda e, stage=stage, ci=cg * 4 + c4, dst_d=dst_d: e.dma_start(out=dst_d[ci], in_=stage[:, 0:S]),
                      reads=[skey], writes=[(dname, cg * 4 + c4)])
            more_mod()
    more_mod(24)

    if upto == "B":
        return finish()

    R.barrier()
    AR.reset(base_mark)
    ctab = AR.alloc([128, 6, 128], F32)
    pcol = AR.alloc([128, 2], F32)
    dec2 = AR.alloc([128, 16], F32)
    e1 = AR.alloc([128, 16], F32)
    l1 = AR.alloc([128, 16], F32)
    lg = AR.alloc([128, 16], F32)
    kdecf = AR.alloc([128, 8], F32)
    kdecb = AR.alloc([128, 8], F32)
    gfb = AR.alloc([128, 16], F32)
    qdecf = AR.alloc([128, 8, 128], F32)
    qdecb = AR.alloc([128, 8, 128], F32)
    maskT = AR.alloc([128, 8, 128], F32)
    ef = AR.alloc([128, 128], F32)
    eb = AR.alloc([128, 128], F32)
    m1 = AR.alloc([128, 128], F32)
    m2 = AR.alloc([128, 128], F32)
    R.dma("sp", lambda e: e.dma_start(out=ctab, in_=ctab_d), writes=["ctab"])
    R.dma("sp", lambda e: e.dma_start(out=pcol, in_=pcol_d), writes=["pcol"])
    R.dma("sp", lambda e: e.dma_start(out=dec2, in_=decay.partition_broadcast(128)), writes=["dec2"])
    act(e1, dec2, AF.Exp, ["dec2"], ["e1"], scale=-1.0)
    act(l1, e1, AF.Ln, ["e1"], ["l1"], bias=eps_t[:, 1:2])
    R.dve(lambda e: e.tensor_scalar_mul(out=lg, in0=l1, scalar1=-1.0), reads=["l1"], writes=["lg"])
    act(kdecf, lg[:, 0:8], AF.Exp, ["lg", "pcol"], ["kdec"], scale=pcol[:, 0:1])
    act(kdecb, lg[:, 8:16], AF.Exp, ["lg", "pcol"], ["kdec"], scale=pcol[:, 1:2])
    act(gfb, lg, AF.Exp, ["lg"], ["gfb"], scale=128.0)
    for h in range(8):
        act(qdecf[:, h, :], ctab[:, 0, :], AF.Exp, ["lg", "ctab"], ["qdec"], scale=lg[:, h:h + 1])
        act(qdecb[:, h, :], ctab[:, 1, :], AF.Exp, ["lg", "ctab"], ["qdec"], scale=lg[:, 8 + h:9 + h])
        act(ef, ctab[:, 2, :], AF.Exp, ["lg", "ctab"], ["ef"], scale=lg[:, h:h + 1])
        act(eb, ctab[:, 4, :], AF.Exp, ["lg", "ctab"], ["eb"], scale=lg[:, 8 + h:9 + h])
        R.dve(lambda e: e.tensor_tensor(out=m1, in0=ef, in1=ctab[:, 3, :], op=ALU.mult), reads=["ef", "ctab"], writes=["m1"])
        R.dve(lambda e: e.tensor_tensor(out=m2, in0=eb, in1=ctab[:, 5, :], op=ALU.mult), reads=["eb", "ctab"], writes=["m2"])
        R.dve(lambda e, h=h: e.tensor_tensor(out=maskT[:, h, :], in0=m1, in1=m2, op=ALU.add), reads=["m1", "m2"], writes=["maskT"])

    KTr = Ring("KT", [AR.alloc([128, T], BF16) for _ in range(3)])
    QTr = Ring("QT", [AR.alloc([128, S], BF16) for _ in range(3)])
    Vr = Ring("V", [AR.alloc([128, NT, 256], BF16) for _ in range(3)])
    RGr = Ring("RG", [AR.alloc([128, 16, 256], BF16) for _ in range(3)])
    QfTr = Ring("QfT", [AR.alloc([128, S], BF16) for _ in range(2)])
    QbTr = Ring("QbT", [AR.alloc([128, S], BF16) for _ in range(2)])
    Kf = AR.alloc([128, NT, 128], BF16)
    Kb = AR.alloc([128, NT, 128], BF16)
    Rfbr = Ring("Rfb", [AR.alloc([128, 16, 256], BF16) for _ in range(2)])
    Rbbr = Ring("Rbb", [AR.alloc([128, 16, 256], BF16) for _ in range(2)])
    Rf = [AR.alloc([128, 256], F32) for _ in range(2)]
    Rb = [AR.alloc([128, 256], F32) for _ in range(2)]
    Pmr = Ring("Pm", [AR.alloc([128, 16, 128], BF16) for _ in range(2)])
    ogs_ring = Ring("ogs", [AR.alloc([128, 4, 256], BF16) for _ in range(2)])
    ogT_ring = Ring("ogTs", [AR.alloc([128, 2, S], BF16) for _ in range(2)])
    junkC = AR.alloc([128, 256], BF16)
    ssqC = AR.alloc([128, 128], F32)
    rsC = AR.alloc([128, 128], F32)
    rstdC = AR.alloc([128, 128], F32)
    psK = Ring("ps", [0, 1])
    psU = Ring("ps", [2, 3])
    psO = Ring("ps", [4, 5])
    psT = Ring("ps", [6, 7])
    v_v = v_d.rearrange("(t p) c -> p t c", p=128)
    rg_v = rgs_d.rearrange("(t p) c -> p t c", p=128)
    CH = {}

    LD = {}

    def loads(h):
        KT, kkey = KTr.next()
        QT, qkey = QTr.next()
        V, vkey = Vr.next()
        RG, rgkey = RGr.next()
        R.dma("sp", lambda e: e.dma_start(out=KT, in_=kT_d[h]), writes=[kkey])
        R.dma("sp", lambda e: e.dma_start(out=QT, in_=qT_d[h]), writes=[qkey])
        R.dma("sp", lambda e: e.dma_start(out=V, in_=v_v[:, :, h * 256:(h + 1) * 256]), writes=[vkey])
        R.dma("sp", lambda e: e.dma_start(out=RG, in_=rg_v[:, :, h * 256:(h + 1) * 256]), writes=[rgkey])
        LD[h] = (KT, kkey, QT, qkey, V, vkey, RG, rgkey)

    def stage1(h):
        KT, kkey, QT, qkey, V, vkey, RG, rgkey = LD[h]
        QfT, qfkey = QfTr.next()
        QbT, qbkey = QbTr.next()
        Rfb, rfbkey = Rfbr.next()
        Rbb, rbbkey = Rbbr.next()
        Pm, pmkey = Pmr.next()
        CH[h] = dict(V=V, vkey=vkey, RG=RG, rgkey=rgkey, QfT=QfT, qfkey=qfkey, QbT=QbT, qbkey=qbkey, Rfb=Rfb, rfbkey=rfbkey,
                     Rbb=Rbb, rbbkey=rbbkey, Pm=Pm, pmkey=pmkey)
        QTv = QT.rearrange("p (c n) -> p c n", c=16)
        qpieces = []
        for (dst, tab, dkey) in ((QfT, qdecf, qfkey), (QbT, qdecb, qbkey)):
            dstv = dst.rearrange("p (c n) -> p c n", c=16)
            for q_ in range(4):
                qpieces.append(lambda dstv=dstv, tab=tab, dkey=dkey, q_=q_: R.dve(
                    lambda e: e.tensor_tensor(out=dstv[:, q_ * 4:(q_ + 1) * 4, :], in0=QTv[:, q_ * 4:(q_ + 1) * 4, :],
                                              in1=tab[:, h, :].unsqueeze(1).broadcast_to([128, 4, 128]), op=ALU.mult),
                    reads=[qkey, "qdec"], writes=[dkey]))
        yield
        for (a_, b_) in ((0, 8), (8, 16), (16, 18)):
            bi, _ = psK.next()
            for i, cch in enumerate(range(a_, b_)):
                tr(psb[bi][:, i * 128:(i + 1) * 128], KT[:, cch * 128:(cch + 1) * 128], identb, [kkey, "identb"], PK(bi))
            n_ = b_ - a_
            src = psb[bi][:, 0:n_ * 128].rearrange("p (c d) -> p c d", c=n_)
            act(Kf[:, a_:b_, :], src, AF.Identity, ["kdec"], [PK(bi), "Kf"], scale=kdecf[:, h:h + 1])
            act(Kb[:, a_:b_, :], src, AF.Identity, ["kdec"], [PK(bi), "Kb"], scale=kdecb[:, h:h + 1])
            yield

        def U(Kd, kdkey, idx):
            bi, _ = psU.next()
            mm(ps[bi][:, 0:256], Kd[:, idx, :], V[:, idx, :], True, True, [kdkey, vkey], PK(bi))
            return bi

        def step(Rl, rname, p, bi, gcol, first):
            if first:
                R.dve(lambda e: e.tensor_copy(out=Rl[0], in_=ps[bi][:, 0:256]), reads=[], writes=[PK(bi), (rname, 0)])
                return 0
            q = 1 - p
            R.dve(lambda e: e.scalar_tensor_tensor(out=Rl[q], in0=Rl[p], scalar=gfb[:, gcol:gcol + 1], in1=ps[bi][:, 0:256],
                                                   op0=ALU.mult, op1=ALU.add),
                  reads=[(rname, p), "gfb"], writes=[PK(bi), (rname, q)])
            return q

        ford = [0, 1] + [2 + n for n in range(15)]
        bord = [1, 0] + [2 + n for n in range(15, 0, -1)]
        pf = pb_ = 0
        pend = []
        for i in range(17):
            for fn_ in pend:
                fn_()
            pend = []
            if qpieces:
                qpieces.pop(0)()
            bi = U(Kf, "Kf", ford[i])
            pf = step(Rf, "Rf", pf, bi, h, i == 0)
            if i >= 1:
                n = i - 1
                pend.append(lambda pf=pf, n=n: R.act(lambda e: e.copy(out=Rfb[:, n, :], in_=Rf[pf]), reads=[("Rf", pf)], writes=[(rfbkey, n)]))
            bi = U(Kb, "Kb", bord[i])
            pb_ = step(Rb, "Rb", pb_, bi, 8 + h, i == 0)
            if i >= 1:
                n = 16 - i
                pend.append(lambda pb_=pb_, n=n: R.act(lambda e: e.copy(out=Rbb[:, n, :], in_=Rb[pb_]), reads=[("Rb", pb_)], writes=[(rbbkey, n)]))
            if i % 4 == 3 and i < 16:
                n0 = (i // 4) * 4
                bs, _ = psK.next()
                for c in range(4):
                    n = n0 + c
                    mm(ps[bs][:, c * 128:(c + 1) * 128], KT[:, (2 + n) * 128:(3 + n) * 128], QT[:, n * 128:(n + 1) * 128], True, True,
                       [kkey, qkey], PK(bs))
                R.dve(lambda e, bs=bs, n0=n0: e.tensor_tensor(out=Pm[:, n0:n0 + 4, :], in0=ps[bs].rearrange("p (c n) -> p c n", c=4),
                                                              in1=maskT[:, h, :].unsqueeze(1).broadcast_to([128, 4, 128]), op=ALU.mult),
                      reads=["maskT"], writes=[PK(bs), (pmkey, n0)])
            yield
        for fn_ in pend:
            fn_()

    def stage2(h):
        d = CH[h]
        V, vkey, RG, rgkey = d["V"], d["vkey"], d["RG"], d["rgkey"]
        QfT, QbT, Rfb, Rbb, Pm = d["QfT"], d["QbT"], d["Rfb"], d["Rbb"], d["Pm"]
        ogT_s, ogTkey = ogT_ring.next()
        ogs = ogskey = None
        pend2 = None
        for n in range(16):
            bo, _ = psO.next()
            o_ap = ps[bo][:, 0:256]
            mm(o_ap, Pm[:, n, :], V[:, 2 + n, :], True, False, [(d["pmkey"], n // 4 * 4), vkey], PK(bo))
            mm(o_ap, QfT[:, n * 128:(n + 1) * 128], Rfb[:, n, :], False, False, [d["qfkey"], (d["rfbkey"], n)], PK(bo))
            mm(o_ap, QbT[:, n * 128:(n + 1) * 128], Rbb[:, n, :], False, True, [d["qbkey"], (d["rbbkey"], n)], PK(bo))
            si = h * 16 + n
            act(junkC, o_ap, AF.Square, [], [PK(bo), "junkC", ("ssqC", si)], accum_out=ssqC[:, si:si + 1])
            rstd_from_ssq(ssqC[:, si:si + 1], rsC[:, si:si + 1], rstdC[:, si:si + 1], 256, ("ssqC", si), ("rstdC", si))
            if n % 4 == 0:
                ogs, ogskey = ogs_ring.next()
            R.dve(lambda e, o_ap=o_ap, si=si, ogs=ogs, n=n: e.scalar_tensor_tensor(
                out=ogs[:, n % 4, :], in0=o_ap, scalar=rstdC[:, si:si + 1], in1=RG[:, n, :], op0=ALU.mult, op1=ALU.mult),
                reads=[("rstdC", si), rgkey], writes=[PK(bo), ogskey])
            if pend2 is not None:
                pend2()
                pend2 = None
            if n % 4 == 3:
                def _tr(n0=n - 3, ogs=ogs, ogskey=ogskey):
                    bt, _ = psT.next()
                    for ec in range(2):
                        for c in range(4):
                            tr(psb[bt][:, (ec * 4 + c) * 128:(ec * 4 + c + 1) * 128], ogs[:, c, ec * 128:(ec + 1) * 128], identb,
                               [ogskey, "identb"], PK(bt))
                    R.act(lambda e: e.copy(out=ogT_s[:, :, n0 * 128:(n0 + 4) * 128],
                                           in_=psb[bt].rearrange("p (a n) -> p a n", a=2)),
                          reads=[], writes=[PK(bt), ogTkey])
                pend2 = _tr
            yield
        if pend2 is not None:
            pend2()
        R.dma("sp", lambda e: e.dma_start(out=ogT_d[2 * h:2 * h + 2].rearrange("k p n -> p k n"), in_=ogT_s),
              reads=[ogTkey], writes=[("ogT_d", h)])

    loads(0)
    loads(1)
    for _ in stage1(0):
        pass
    for h in range(8):
        if h + 2 < 8:
            loads(h + 2)
        g2 = stage2(h)
        g1 = stage1(h + 1) if h + 1 < 8 else iter(())
        done1 = done2 = False
        while not (done1 and done2):
            if not done1:
                try:
                    next(g1)
                except StopIteration:
                    done1 = True
            if not done2:
                try:
                    next(g2)
                except StopIteration:
                    done2 = True

    if upto == "C":
        return finish()

    R.barrier()
    AR.reset(base_mark)
    kvnT = AR.alloc([128, 4, T], BF16)
    qnT = AR.alloc([128, 4, S], BF16)
    krT2 = AR.alloc([128, T], BF16)
    cosM = AR.alloc([128, S], F32)
    sinM = AR.alloc([128, S], F32)
    kvn_v = kvnT_d.rearrange("k p n -> p k n")
    qn_v = qnT_d.rearrange("k p n -> p k n")
    for i_, (t0_, n_) in enumerate(TG):
        R.dma("sp", lambda e, t0_=t0_, n_=n_: e.dma_start(out=kvnT[:, :, t0_:t0_ + n_], in_=kvn_v[:, :, t0_:t0_ + n_]), writes=[("kvnT", i_)])
    for g_ in range(4):
        R.dma("sp", lambda e, g_=g_: e.dma_start(out=qnT[:, :, g_ * 512:(g_ + 1) * 512], in_=qn_v[:, :, g_ * 512:(g_ + 1) * 512]), writes=[("qnT", g_)])

    def kvn_piece(tile):
        return 0 if tile < 2 else 1 + (tile - 2) // 4
    R.dma("sp", lambda e: e.dma_start(out=krT2[0:64, :], in_=krT_d), writes=["krT"])
    R.dma("sp", lambda e: e.dma_start(out=krT2[64:128, :], in_=krT_d), writes=["krT"])
    R.dma("sp", lambda e: e.dma_start(out=cosM, in_=ropeM_d[0]), writes=["ropeM"])
    R.dma("sp", lambda e: e.dma_start(out=sinM, in_=ropeM_d[1]), writes=["ropeM"])
    PT_ring = Ring("PT", [AR.alloc([128, NT, 512], BF16) for _ in range(3)])
    wkv_r = Ring("wkv", [AR.alloc([128, 4, 512], BF16) for _ in range(1)])
    wq_r = Ring("wq", [AR.alloc([128, 4, 384], BF16) for _ in range(1)])
    wqr_r = Ring("wqr", [AR.alloc([128, 512], BF16) for _ in range(1)])
    wqrsw_r = Ring("wqrsw", [AR.alloc([128, 512], BF16) for _ in range(1)])
    KnT_r = Ring("KnT", [AR.alloc([128, T], BF16) for _ in range(3)])
    Vp_r = Ring("Vp", [AR.alloc([128, NT, 129], BF16) for _ in range(3)])
    QnT_r = Ring("QnT", [AR.alloc([128, S], BF16) for _ in range(3)])
    QrA_r = Ring("QrA", [AR.alloc([128, S], BF16) for _ in range(2)])
    QrB_r = Ring("QrB", [AR.alloc([128, S], BF16) for _ in range(2)])
    On_r = Ring("On", [AR.alloc([128, 16, 128], BF16) for _ in range(2)])
    omT_r = Ring("omTs", [AR.alloc([128, S], BF16) for _ in range(2)])
    t1d = AR.alloc([128, 512], F32)
    t2d = AR.alloc([128, 512], F32)
    rden = AR.alloc([128, 32], F32)
    for _k, vp in enumerate(Vp_r.items):
        R.dve(lambda e, vp=vp: e.memset(vp[:, :, 128:129], 1.0), writes=[("Vp", _k)])
    for _k in range(2):
        R.dve(lambda e, _k=_k: e.memset(QrA_r.items[_k][64:128, :], 0.0), writes=[("QrA", _k)])
        R.dve(lambda e, _k=_k: e.memset(QrB_r.items[_k][0:64, :], 0.0), writes=[("QrB", _k)])
    psP = Ring("ps", [0, 1])
    psS = Ring("ps", [2, 3, 4])
    psO = Ring("ps", [5, 6])
    psT = Ring("ps", [7])
    att_scale = 192.0 ** -0.5
    HD = {}

    def proj_pair(p):
        h0 = 2 * p
        wkv, wkvkey = wkv_r.next()
        wq, wqkey = wq_r.next()
        wqr, wqrkey = wqr_r.next()
        wqrsw, wqrswkey = wqrsw_r.next()
        R.dma("pool", lambda e: e.dma_start(out=wkv, in_=w_kv_up[:, h0 * 256:(h0 + 2) * 256].rearrange("(k p) n -> p k n", p=128)), writes=[wkvkey])
        R.dma("pool", lambda e: e.dma_start(out=wq, in_=w_q_up[:, h0 * 192:(h0 + 2) * 192].rearrange("(k p) n -> p k n", p=128)), writes=[wqkey])
        wqr3 = wqr.rearrange("p (k n) -> p k n", k=4)
        R.dve(lambda e: e.tensor_copy(out=wqr3[:, :, 0:64], in_=wq[:, :, 128:192]), reads=[wqkey], writes=[wqrkey])
        R.dve(lambda e: e.tensor_copy(out=wqr3[:, :, 64:128], in_=wq[:, :, 320:384]), reads=[wqkey], writes=[wqrkey])
        sv = wqr.rearrange("p (g b m) -> p g b m", b=2, m=16)
        dv = wqrsw.rearrange("p (g b m) -> p g b m", b=2, m=16)
        R.dve(lambda e: e.tensor_copy(out=dv[:, :, 0, :], in_=sv[:, :, 1, :]), reads=[wqrkey], writes=[wqrswkey])
        R.dve(lambda e: e.tensor_copy(out=dv[:, :, 1, :], in_=sv[:, :, 0, :]), reads=[wqrkey], writes=[wqrswkey])
        wqrsw3 = wqrsw.rearrange("p (k n) -> p k n", k=4)
        QrA, qrakey = QrA_r.next()
        QrB, qrbkey = QrB_r.next()
        for hh in range(2):
            h = h0 + hh
            KnT, knkey = KnT_r.next()
            Vp, vpkey = Vp_r.next()
            QnT, qnkey = QnT_r.next()
            On, onkey = On_r.next()
            omT_s, omkey = omT_r.next()
            HD[h] = dict(KnT=KnT, knkey=knkey, Vp=Vp, vpkey=vpkey, QnT=QnT, qnkey=qnkey, On=On, onkey=onkey, omT_s=omT_s, omkey=omkey,
                         Qr=(QrA if hh == 0 else QrB), qrkey=(qrakey if hh == 0 else qrbkey), PTs={})
            kc0 = hh * 256
            for ti_, (t0, n) in enumerate(TG):
                bi, _ = psP.next()
                for kc in range(4):
                    mm(ps[bi][:, 0:n], wkv[:, kc, kc0:kc0 + 128], kvnT[:, kc, t0:t0 + n], kc == 0, kc == 3, [wkvkey, ("kvnT", ti_)], PK(bi))
                R.dve(lambda e, bi=bi, t0=t0, n=n, KnT=KnT: e.tensor_copy(out=KnT[:, t0:t0 + n], in_=ps[bi][:, 0:n]), reads=[], writes=[PK(bi), knkey])
            for t0 in range(0, NT, 4):
                ncn = min(4, NT - t0)
                bi, _ = psP.next()
                for c in range(ncn):
                    t = t0 + c
                    for kc in range(4):
                        mm(ps[bi][:, c * 128:(c + 1) * 128], kvnT[:, kc, t * 128:(t + 1) * 128], wkv[:, kc, kc0 + 128:kc0 + 256], kc == 0, kc == 3,
                           [wkvkey, ("kvnT", kvn_piece(t))], PK(bi))
                R.dve(lambda e, bi=bi, t0=t0, ncn=ncn, Vp=Vp: e.tensor_copy(out=Vp[:, t0:t0 + ncn, 0:128],
                                                                            in_=ps[bi][:, 0:ncn * 128].rearrange("p (c d) -> p c d", c=ncn)),
                      reads=[], writes=[PK(bi), vpkey])
            qc0 = hh * 192
            for g in range(4):
                bi, _ = psP.next()
                for kc in range(4):
                    mm(ps[bi][:, :], wq[:, kc, qc0:qc0 + 128], qnT[:, kc, g * 512:(g + 1) * 512], kc == 0, kc == 3, [wqkey, ("qnT", g)], PK(bi))
                R.dve(lambda e, bi=bi, g=g, QnT=QnT: e.tensor_copy(out=QnT[:, g * 512:(g + 1) * 512], in_=ps[bi][:, :]), reads=[], writes=[PK(bi), qnkey])
        for g in range(4):
            bi, _ = psP.next()
            for kc in range(4):
                mm(ps[bi][:, :], wqr3[:, kc, :], qnT[:, kc, g * 512:(g + 1) * 512], kc == 0, kc == 3, [wqrkey, ("qnT", g)], PK(bi))
            bj, _ = psP.next()
            for kc in range(4):
                mm(ps[bj][:, :], wqrsw3[:, kc, :], qnT[:, kc, g * 512:(g + 1) * 512], kc == 0, kc == 3, [wqrswkey, ("qnT", g)], PK(bj))
            R.dve(lambda e, bi=bi, g=g: e.tensor_tensor(out=t1d, in0=ps[bi][:, :], in1=cosM[:, g * 512:(g + 1) * 512], op=ALU.mult),
                  reads=["ropeM"], writes=[PK(bi), "t1d"])
            R.dve(lambda e, bj=bj, g=g: e.tensor_tensor(out=t2d, in0=ps[bj][:, :], in1=sinM[:, g * 512:(g + 1) * 512], op=ALU.mult),
                  reads=["ropeM"], writes=[PK(bj), "t2d"])
            R.dve(lambda e, g=g: e.tensor_tensor(out=QrA[0:64, g * 512:(g + 1) * 512], in0=t1d[0:64, :], in1=t2d[0:64, :], op=ALU.add),
                  reads=["t1d", "t2d"], writes=[qrakey])
            R.dve(lambda e, g=g: e.tensor_tensor(out=QrB[64:128, g * 512:(g + 1) * 512], in0=t1d[64:128, :], in1=t2d[64:128, :], op=ALU.add),
                  reads=["t1d", "t2d"], writes=[qrbkey])

    def scores(h, g):
        d = HD[h]
        PTb, ptkey = PT_ring.next()
        d["PTs"][g] = (PTb, ptkey)
        for j in range(NT):
            bi, _ = psS.next()
            mm(ps[bi][:, :], d["KnT"][:, j * 128:(j + 1) * 128], d["QnT"][:, g * 512:(g + 1) * 512], True, False, [d["knkey"], d["qnkey"]], PK(bi))
            mm(ps[bi][:, :], krT2[:, j * 128:(j + 1) * 128], d["Qr"][:, g * 512:(g + 1) * 512], False, True, ["krT", d["qrkey"]], PK(bi))
            act(PTb[:, j, :], ps[bi][:, :], AF.Exp, [], [PK(bi), (ptkey, j)], scale=att_scale)

    def pv(h, g):
        d = HD[h]
        PTb, ptkey = d["PTs"][g]
        On = d["On"]
        for qs in range(4):
            qb = g * 4 + qs
            bo, _ = psO.next()
            for j in range(NT):
                mm(ps[bo][:, 0:129], PTb[:, j, qs * 128:(qs + 1) * 128], d["Vp"][:, j, :], j == 0, j == NT - 1, [(ptkey, j), d["vpkey"]], PK(bo))
            rc = (h % 2) * 16 + qb
            R.dve(lambda e, bo=bo, rc=rc: e.reciprocal(out=rden[:, rc:rc + 1], in_=ps[bo][:, 128:129]), reads=[], writes=[PK(bo), ("rden", rc)])
            R.dve(lambda e, bo=bo, rc=rc, qb=qb, On=On: e.tensor_scalar(out=On[:, qb, :], in0=ps[bo][:, 0:128], scalar1=rden[:, rc:rc + 1],
                                                                   scalar2=None, op0=ALU.mult),
                  reads=[("rden", rc)], writes=[PK(bo), d["onkey"]])

    def finish_head(h):
        d = HD[h]
        for qb0 in (0, 8):
            bt, _ = psT.next()
            for c in range(8):
                tr(psb[bt][:, c * 128:(c + 1) * 128], d["On"][:, qb0 + c, :], identb, [d["onkey"], "identb"], PK(bt))
            R.act(lambda e, bt=bt, qb0=qb0, omT_s=d["omT_s"]: e.copy(out=omT_s[:, qb0 * 128:(qb0 + 8) * 128], in_=psb[bt][:, :]),
                  reads=[], writes=[PK(bt), d["omkey"]])
        R.dma("sp", lambda e, h=h, omT_s=d["omT_s"]: e.dma_start(out=omT_d[h], in_=omT_s), reads=[d["omkey"]], writes=[("omT_d", h)])

    items = [(h, g) for h in range(16) for g in range(4)]
    proj_pair(0)
    scores(*items[0])
    for i, (h, g) in enumerate(items):
        if i + 1 < len(items):
            h2, g2 = items[i + 1]
            if g2 == 0 and h2 % 2 == 0:
                proj_pair(h2 // 2)
            scores(h2, g2)
        pv(h, g)
        if g == 3:
            finish_head(h)

    if upto == "D":
        return finish()

    R.barrier()
    AR.reset(base_mark)
    ogT = AR.alloc([128, 16, S], BF16)
    omT = AR.alloc([128, 16, S], BF16)
    for q4_ in range(4):
        R.dma("sp", lambda e, q4_=q4_: e.dma_start(out=ogT[:, :, q4_ * 512:(q4_ + 1) * 512],
                                                 in_=ogT_d[:, :, q4_ * 512:(q4_ + 1) * 512].rearrange("k p n -> p k n")),
              writes=[("ogT", q4_)])
        R.dma("sp", lambda e, q4_=q4_: e.dma_start(out=omT[:, :, q4_ * 512:(q4_ + 1) * 512],
                                                 in_=omT_d[:, :, q4_ * 512:(q4_ + 1) * 512].rearrange("k p n -> p k n")),
              writes=[("omT", q4_)])
    wr_ring = Ring("wr", [AR.alloc([128, 4096], BF16) for _ in range(2)])
    wm_ring = Ring("wm", [AR.alloc([128, 4096], BF16) for _ in range(2)])
    gr_ring = Ring("grt", [AR.alloc([128, S], BF16) for _ in range(2)])
    gm_ring = Ring("gmt", [AR.alloc([128, S], BF16) for _ in range(2)])
    t1e = AR.alloc([128, 512], F32)
    t2e = AR.alloc([128, 512], F32)
    mix_ring = Ring("mixs", [AR.alloc([128, S], BF16) for _ in range(2)])
    psr = Ring("ps", list(range(8)))
    gates = {}

    def load_gates(c):
        grt, grkey = gr_ring.next()
        gmt, gmkey = gm_ring.next()
        R.dma("sp", lambda e: e.dma_start(out=grt, in_=gr_d[c]), writes=[grkey])
        R.dma("sp", lambda e: e.dma_start(out=gmt, in_=gm_d[c]), writes=[gmkey])
        gates[c] = (grt, grkey, gmt, gmkey)

    load_gates(0)
    for cg in range(8):
        wr, wrkey, _ = wload(wr_ring, w_ret_o[:, cg * 256:(cg + 1) * 256], 16, 256)
        wm, wmkey, _ = wload(wm_ring, w_mla_o[:, cg * 256:(cg + 1) * 256], 16, 256)
        for c2 in range(2):
            c = cg * 2 + c2
            if c + 1 < 16:
                load_gates(c + 1)
            grt, grkey, gmt, gmkey = gates[c]
            mixs, mkey = mix_ring.next()
            for g in range(4):
                bi, _ = psr.next()
                for ec in range(16):
                    mm(ps[bi][:, :], wr[:, ec, c2 * 128:(c2 + 1) * 128], ogT[:, ec, g * 512:(g + 1) * 512], ec == 0, ec == 15,
                       [wrkey[ec], ("ogT", g)], PK(bi))
                bj, _ = psr.next()
                for ec in range(16):
                    mm(ps[bj][:, :], wm[:, ec, c2 * 128:(c2 + 1) * 128], omT[:, ec, g * 512:(g + 1) * 512], ec == 0, ec == 15,
                       [wmkey[ec], ("omT", g)], PK(bj))
                R.dve(lambda e, bi=bi, g=g, grt=grt: e.tensor_tensor(out=t1e, in0=ps[bi][:, :], in1=grt[:, g * 512:(g + 1) * 512], op=ALU.mult),
                      reads=[grkey], writes=[PK(bi), "t1e"])
                R.dve(lambda e, bj=bj, g=g, gmt=gmt: e.tensor_tensor(out=t2e, in0=ps[bj][:, :], in1=gmt[:, g * 512:(g + 1) * 512], op=ALU.mult),
                      reads=[gmkey], writes=[PK(bj), "t2e"])
                R.dve(lambda e, g=g, mixs=mixs: e.tensor_tensor(out=mixs[:, g * 512:(g + 1) * 512], in0=t1e, in1=t2e, op=ALU.add),
                      reads=["t1e", "t2e"], writes=[mkey])
            R.dma("sp", lambda e, mixs=mixs, c=c: e.dma_start(out=mixT_d[c], in_=mixs), reads=[mkey], writes=[("mixT_d", c)])

    if upto == "E":
        return finish()

    R.barrier()
    AR.reset(base_mark)
    mixT = AR.alloc([128, 16, S], BF16)
    for q4_ in range(4):
        R.dma("sp", lambda e, q4_=q4_: e.dma_start(out=mixT[:, :, q4_ * 512:(q4_ + 1) * 512],
                                                 in_=mixT_d[:, :, q4_ * 512:(q4_ + 1) * 512].rearrange("k p n -> p k n")),
              writes=[("mixT", q4_)])
    wo_ring = Ring("wo", [AR.alloc([128, 8192], BF16) for _ in range(2)])
    g1b = AR.alloc([128, D], F32)
    bcast_load(g1b, modrow(0, 2), "g1b")
    xr_ring = Ring("xr", [AR.alloc([128, 512], F32) for _ in range(3)])
    tF_ring = Ring("tF", [AR.alloc([128, 512], F32) for _ in range(2)])
    x1s_ring = Ring("x1s", [AR.alloc([128, 512], F32) for _ in range(3)])
    psr = Ring("ps", list(range(8)))
    xrs = {}

    def load_xr(k):
        cg_, t_ = divmod(k, 16)
        xr, xrkey = xr_ring.next()
        R.dma("sp", lambda e: e.dma_start(out=xr, in_=x[t_ * 128:(t_ + 1) * 128, cg_ * 512:(cg_ + 1) * 512]), writes=[xrkey])
        xrs[k] = (xr, xrkey)

    load_xr(0)
    load_xr(1)
    for cg in range(4):
        wo, wokey, _ = wload(wo_ring, w_out[:, cg * 512:(cg + 1) * 512], 16, 512)
        for t in range(16):
            k = cg * 16 + t
            if k + 2 < 64:
                load_xr(k + 2)
            xr, xrkey = xrs[k]
            bi, _ = psr.next()
            for c in range(16):
                mm(ps[bi][:, :], mixT[:, c, t * 128:(t + 1) * 128], wo[:, c, :], c == 0, c == 15, [wokey[c], ("mixT", t // 4)], PK(bi))
            tF, tFkey = tF_ring.next()
            x1s, x1key = x1s_ring.next()
            R.dve(lambda e, bi=bi, cg=cg, tF=tF: e.tensor_tensor(out=tF, in0=ps[bi][:, :], in1=g1b[:, cg * 512:(cg + 1) * 512], op=ALU.mult),
                  reads=["g1b"], writes=[PK(bi), tFkey])
            R.dve(lambda e, tF=tF, xr=xr, x1s=x1s: e.tensor_tensor(out=x1s, in0=tF, in1=xr, op=ALU.add), reads=[tFkey, xrkey], writes=[x1key])
            R.dma("sp", lambda e, x1s=x1s, t=t, cg=cg: e.dma_start(out=x1_d[t * 128:(t + 1) * 128, cg * 512:(cg + 1) * 512], in_=x1s),
                  reads=[x1key], writes=[("x1_d", t, cg)])

    if upto == "F":
        return finish()

    R.barrier()
    AR.reset(base_mark)
    ssq2 = AR.alloc([128, 128], F32)
    ssqT = AR.alloc([128, 16], F32)
    rsT = AR.alloc([128, 16], F32)
    rstdT = AR.alloc([128, 16], F32)
    convw = AR.alloc([128, 3, NFC], F32)
    convb = AR.alloc([128, NFC], F32)
    g2b = AR.alloc([128, D], F32)
    gT = AR.alloc([128, NFC, 1024], BF16)
    R.dma("sp", lambda e: e.dma_start(out=convw, in_=conv_w), writes=["convw"])
    R.dma("sp", lambda e: e.dma_start(out=convb, in_=conv_b), writes=["convb"])
    bcast_load(g2b, modrow(0, 5), "g2b")
    markG = AR.mark()
    for hf in range(2):
        R.barrier()
        AR.reset(markG)
        h2T = AR.alloc([128, 16, 1152], BF16)
        markG1 = AR.mark()
        sc2p = AR.alloc([128, D], F32)
        sh2b = AR.alloc([128, D], F32)
        xringG = Ring("xtG", [AR.alloc([128, D], F32) for _ in range(2)])
        tmpG = AR.alloc([128, D], F32)
        junkG = AR.alloc([128, D], BF16)
        hbringG = Ring("hbG", [AR.alloc([128, D], BF16) for _ in range(2)])
        ssqG = AR.alloc([128, 16], F32)
        rsG = AR.alloc([128, 16], F32)
        rstdG = AR.alloc([128, 16], F32)
        psrG = Ring("ps", [0, 1, 2, 3])
        bufsG = (xringG, junkG, tmpG, hbringG, psrG)
        bcast_load(sc2p, modrow(0, 4), "sc2p")
        bcast_load(sh2b, modrow(0, 3), "sh2b")
        R.dve(lambda e: e.tensor_scalar_add(out=sc2p, in0=sc2p, scalar1=1.0), reads=["sc2p"], writes=["sc2p"])
        tiles = [hf * 8 + i for i in range(8)] + [8 if hf == 0 else 7]
        run_tiles([(lambda i=i, tt=tt: norm_mod_T(x1_d[tt * 128:(tt + 1) * 128, :], h2T, i * 128, i, sc2p, sh2b, ["sc2p", "sh2b"],
                                                  ssqG, rsG, rstdG, ("h2T", i), bufs=bufsG), None) for i, tt in enumerate(tiles)])
        hcol = 1024 if hf == 0 else 1151

        R.barrier()
        AR.reset(markG1)
        wa_ring = Ring("wa", [AR.alloc([128, 4096], BF16) for _ in range(2)])
        wv_ring = Ring("wv", [AR.alloc([128, 4096], BF16) for _ in range(2)])
        asb_ring = Ring("asb", [AR.alloc([128, 1026], F32) for _ in range(2)])
        acc = AR.alloc([128, 1024], F32)
        for (ab, _k) in zip(asb_ring.items, range(2)):
            R.dve(lambda e, ab=ab: e.memset(ab[:, 0:1], 0.0), writes=[("asb", _k)])
            R.dve(lambda e, ab=ab: e.memset(ab[:, 1025:1026], 0.0), writes=[("asb", _k)])
        psr = Ring("ps", list(range(8)))
        h2keys = [("h2T", i) for i in range(9)]
        for f2 in range(NFC // 2):
            wa, wakey, _ = wload(wa_ring, w_up[:, f2 * 256:(f2 + 1) * 256], 16, 256)
            wv, wvkey, _ = wload(wv_ring, w_up[:, FFN + f2 * 256:FFN + (f2 + 1) * 256], 16, 256)
            for fi in range(2):
                fc = f2 * 2 + fi
                asb, asbkey = asb_ring.next()
                ba = []
                for tg in range(2):
                    bi, _ = psr.next()
                    ba.append(bi)
                    for kc in range(16):
                        mm(ps[bi][:, :], wa[:, kc, fi * 128:(fi + 1) * 128], h2T[:, kc, tg * 512:(tg + 1) * 512], kc == 0, kc == 15,
                           [wakey[kc]] + h2keys, PK(bi))
                bh, _ = psr.next()
                for kc in range(16):
                    mm(ps[bh][:, 0:1], wa[:, kc, fi * 128:(fi + 1) * 128], h2T[:, kc, hcol:hcol + 1], kc == 0, kc == 15, [wakey[kc]] + h2keys, PK(bh))
                bv = []
                for tg in range(2):
                    bj, _ = psr.next()
                    bv.append(bj)
                    for kc in range(16):
                        mm(ps[bj][:, :], wv[:, kc, fi * 128:(fi + 1) * 128], h2T[:, kc, tg * 512:(tg + 1) * 512], kc == 0, kc == 15,
                           [wvkey[kc]] + h2keys, PK(bj))
                for tg in range(2):
                    R.act(lambda e, bi=ba[tg], tg=tg, asb=asb: e.copy(out=asb[:, 1 + tg * 512:1 + (tg + 1) * 512], in_=ps[bi][:, :]),
                          reads=[], writes=[PK(ba[tg]), asbkey])
                hdst = 1025 if hf == 0 else 0
                R.act(lambda e, bh=bh, asb=asb, hdst=hdst: e.copy(out=asb[:, hdst:hdst + 1], in_=ps[bh][:, 0:1]), reads=[], writes=[PK(bh), asbkey])
                act(acc, asb[:, 1:1025], AF.Identity, [asbkey, "convw", "convb"], ["acc"], scale=convw[:, 1, fc:fc + 1], bias=convb[:, fc:fc + 1])
                R.dve(lambda e, asb=asb, fc=fc: e.scalar_tensor_tensor(out=acc, in0=asb[:, 0:1024], scalar=convw[:, 0, fc:fc + 1], in1=acc,
                                                                       op0=ALU.mult, op1=ALU.add), reads=[asbkey, "convw", "acc"], writes=["acc"])
                R.dve(lambda e, asb=asb, fc=fc: e.scalar_tensor_tensor(out=acc, in0=asb[:, 2:1026], scalar=convw[:, 2, fc:fc + 1], in1=acc,
                                                                       op0=ALU.mult, op1=ALU.add), reads=[asbkey, "convw", "acc"], writes=["acc"])
                act(acc, acc, AF.Silu, ["acc"], ["acc"])
                for tg in range(2):
                    R.dve(lambda e, bj=bv[tg], tg=tg, fc=fc: e.tensor_tensor(out=gT[:, fc, tg * 512:(tg + 1) * 512], in0=ps[bj][:, :],
                                                                            in1=acc[:, tg * 512:(tg + 1) * 512], op=ALU.mult),
                          reads=["acc"], writes=[PK(bv[tg]), ("gT", fc)])

        R.barrier()
        AR.reset(markG)
        wd_ring = Ring("wd", [AR.alloc([128, NFC * 256], BF16) for _ in range(2)])
        x1r_ring = Ring("x1r", [AR.alloc([128, 256], F32) for _ in range(3)])
        tG_ring = Ring("tG", [AR.alloc([128, 256], F32) for _ in range(2)])
        x2s_ring = Ring("x2s", [AR.alloc([128, 256], F32) for _ in range(3)])
        junk2 = AR.alloc([128, 256], BF16)
        psr = Ring("ps", list(range(8)))
        gkeys = [("gT", fc) for fc in range(NFC)]
        if hf == 1:
            fnbE = AR.alloc([128, D], F32)
            x2tE = AR.alloc([128, D], F32)
            outsE = AR.alloc([128, D], F32)
            bcast_load(fnbE, fnorm, "fnbE")
            R.dve(lambda e: e.tensor_reduce(out=ssqT[:, 0:8], in_=ssq2[:, 0:64].rearrange("p (t c) -> p t c", c=8),
                                            axis=mybir.AxisListType.X, op=ALU.add), reads=[("ssq2", t_) for t_ in range(8)], writes=["ssqT0"])
            act(rsT[:, 0:8], ssqT[:, 0:8], AF.Sqrt, ["ssqT0"], ["rsT0"], scale=1.0 / D, bias=eps_t[:, 0:1])
            R.dve(lambda e: e.reciprocal(out=rstdT[:, 0:8], in_=rsT[:, 0:8]), reads=["rsT0"], writes=["rstdT0"])

            def early_final(tt_):
                R.dma("sp", lambda e: e.dma_start(out=x2tE, in_=x2_d[tt_ * 128:(tt_ + 1) * 128, :]), writes=["x2tE"])
                R.dve(lambda e: e.scalar_tensor_tensor(out=outsE, in0=x2tE, scalar=rstdT[:, tt_:tt_ + 1], in1=fnbE, op0=ALU.mult, op1=ALU.mult),
                      reads=["x2tE", "rstdT0", "fnbE"], writes=["outsE"])
                final_ops.append(R.dma("sp", lambda e: e.dma_start(out=out[tt_ * 128:(tt_ + 1) * 128, :], in_=outsE), reads=["outsE"]))
        x1rs = {}

        def load_x1r(k):
            cg_, t_ = divmod(k, 8)
            tt_ = hf * 8 + t_
            x1r, x1rkey = x1r_ring.next()
            R.dma("sp", lambda e: e.dma_start(out=x1r, in_=x1_d[tt_ * 128:(tt_ + 1) * 128, cg_ * 256:(cg_ + 1) * 256]), writes=[x1rkey])
            x1rs[k] = (x1r, x1rkey)

        load_x1r(0)
        load_x1r(1)
        for cg in range(8):
            wd, wdkey, _ = wload(wd_ring, w_down[:, cg * 256:(cg + 1) * 256], NFC, 256)
            for t in range(8):
                tt = hf * 8 + t
                k = cg * 8 + t
                if k + 2 < 64:
                    load_x1r(k + 2)
                x1r, x1rkey = x1rs[k]
                bi, _ = psr.next()
                for fc in range(NFC):
                    mm(ps[bi][:, 0:256], gT[:, fc, t * 128:(t + 1) * 128], wd[:, fc, :], fc == 0, fc == NFC - 1,
                       [wdkey[fc]] + (gkeys if fc == 0 else []), PK(bi))
                tG, tGkey = tG_ring.next()
                x2s, x2key = x2s_ring.next()
                R.dve(lambda e, bi=bi, cg=cg, tG=tG: e.tensor_tensor(out=tG, in0=ps[bi][:, 0:256], in1=g2b[:, cg * 256:(cg + 1) * 256], op=ALU.mult),
                      reads=["g2b"], writes=[PK(bi), tGkey])
                R.dve(lambda e, tG=tG, x1r=x1r, x2s=x2s: e.tensor_tensor(out=x2s, in0=tG, in1=x1r, op=ALU.add), reads=[tGkey, x1rkey], writes=[x2key])
                si = tt * 8 + cg
                act(junk2, x2s, AF.Square, [x2key], ["junk2", ("ssq2", tt)], accum_out=ssq2[:, si:si + 1])
                R.dma("sp", lambda e, x2s=x2s, tt=tt, cg=cg: e.dma_start(out=x2_d[tt * 128:(tt + 1) * 128, cg * 256:(cg + 1) * 256], in_=x2s),
                      reads=[x2key], writes=[("x2_d", tt)])
                if hf == 1 and k % 8 == 5:
                    early_final(k // 8)

    if upto == "G":
        return finish()

    R.barrier()
    AR.reset(markG)
    fnb = AR.alloc([128, D], F32)
    bcast_load(fnb, fnorm, "fnb")
    x2t_ring = Ring("x2t", [AR.alloc([128, D], F32) for _ in range(3)])
    outs_ring = Ring("outs", [AR.alloc([128, D], F32) for _ in range(3)])
    R.dve(lambda e: e.tensor_reduce(out=ssqT[:, 8:16], in_=ssq2[:, 64:128].rearrange("p (t c) -> p t c", c=8), axis=mybir.AxisListType.X, op=ALU.add),
          reads=[], writes=["ssqT"])
    act(rsT[:, 8:16], ssqT[:, 8:16], AF.Sqrt, ["ssqT"], ["rsT"], scale=1.0 / D, bias=eps_t[:, 0:1])
    R.dve(lambda e: e.reciprocal(out=rstdT[:, 8:16], in_=rsT[:, 8:16]), reads=["rsT"], writes=["rstdT"])
    x2ts = {}

    def load_x2t(tt_):
        x2t, x2tkey = x2t_ring.next()
        R.dma("sp", lambda e: e.dma_start(out=x2t, in_=x2_d[tt_ * 128:(tt_ + 1) * 128, :]), writes=[x2tkey])
        x2ts[tt_] = (x2t, x2tkey)

    load_x2t(8)
    load_x2t(9)
    for tt in range(8, 16):
        if tt + 2 < 16:
            load_x2t(tt + 2)
        x2t, x2tkey = x2ts[tt]
        outs, okey = outs_ring.next()
        R.dve(lambda e, x2t=x2t, outs=outs, tt=tt: e.scalar_tensor_tensor(out=outs, in0=x2t, scalar=rstdT[:, tt:tt + 1], in1=fnb,
                                                                      op0=ALU.mult, op1=ALU.mult),
              reads=[x2tkey, "rstdT", "fnb"], writes=[okey])
        final_ops.append(R.dma("sp", lambda e, outs=outs, tt=tt: e.dma_start(out=out[tt * 128:(tt + 1) * 128, :], in_=outs), reads=[okey]))

    return finish()


_CACHE = {}


def make_in_maps(inputs):
    cst = _consts()
    f = lambda a: np.ascontiguousarray(np.asarray(a, dtype=np.float32))
    c_ctx = f(inputs["c_ctx"])
    shared = {
        "w_ada": f(inputs["w_ada"][0]), "b_ada": f(inputs["b_ada"][0]).reshape(-1),
        "w_in": f(inputs["w_in"][0]),
        "decay": np.concatenate([f(inputs["ret_decay_fwd"][0]), f(inputs["ret_decay_bwd"][0])]).reshape(16),
        "ret_gn": f(inputs["ret_gn"][0]).reshape(-1),
        "w_ret_o": f(inputs["w_ret_o"][0]),
        "mla_q_norm": f(inputs["mla_q_norm"][0]).reshape(-1), "mla_kv_norm": f(inputs["mla_kv_norm"][0]).reshape(-1),
        "w_q_up": f(inputs["w_q_up"][0]), "w_kv_up": f(inputs["w_kv_up"][0]),
        "w_mla_o": f(inputs["w_mla_o"][0]), "w_out": f(inputs["w_out"][0]),
        "ffn_w_up": f(inputs["ffn_w_up"][0]),
        "conv_w": np.ascontiguousarray(f(inputs["ffn_conv_w"][0]).reshape(3, NFC, 128).transpose(2, 0, 1)),
        "conv_b": np.ascontiguousarray(f(inputs["ffn_conv_b"][0]).reshape(NFC, 128).T),
        "ffn_w_down": f(inputs["ffn_w_down"][0]), "final_norm": f(inputs["final_norm"]).reshape(-1),
    }
    shared.update(cst)
    maps = []
    for b in range(8):
        m = dict(shared)
        m["x"] = f(inputs["x"][b])
        m["ctx"] = f(inputs["ctx"][b])
        ccb = np.stack([f(inputs["c"][b]).reshape(16, 128).T, c_ctx.reshape(16, 128).T], axis=2)
        m["cc"] = np.ascontiguousarray(ccb)
        maps.append(m)
    return maps


def kernel(**inputs):
    if "nc" not in _CACHE:
        _CACHE["nc"] = build()
    nc = _CACHE["nc"]
    maps = make_in_maps(inputs)
    res = run_bass_kernel_spmd(nc, maps, core_ids=list(range(8)))
    return np.stack([np.asarray(r["out"], dtype=np.float32) for r in res.results], axis=0)
```

```python
import contextlib
import numpy as np
import concourse.bass as bass
import concourse.mybir as mybir
from concourse.bass_utils import run_bass_kernel_spmd

F32 = mybir.dt.float32
BF16 = mybir.dt.bfloat16
AF = mybir.ActivationFunctionType
ALU = mybir.AluOpType

COMPUTE = ("pe", "act", "dve", "pool")
QUEUES = ("sp", "pool")
NRING = 24

D = 2048
S = 2048
LC = 256
T = S + LC
NT = T // 128
FFN = 5632
NFC = FFN // 128
EPS = 1e-6
IN_COLS = 11328
C_RK, C_RV, C_KVD, C_KR, C_RQ, C_RG, C_QD, C_GR, C_GM = 0, 1024, 3072, 3584, 3648, 4672, 6720, 7232, 9280


class Op:
    __slots__ = ("eng", "fn", "deps", "signal", "value", "sem", "is_dma", "dma_k")

    def __init__(self, eng, fn, is_dma):
        self.eng = eng
        self.fn = fn
        self.deps = []
        self.signal = False
        self.value = None
        self.sem = None
        self.is_dma = is_dma
        self.dma_k = None


class Rec:
    def __init__(self):
        self.ops = {e: [] for e in ("pe", "act", "dve", "sp", "pool")}
        self.last_w = {}
        self.readers = {}
        self.dma_ops = {q: [] for q in QUEUES}
        self.last_real = {e: None for e in COMPUTE}
        self.bar_dma_start = {q: 0 for q in QUEUES}

    def _add(self, eng, fn, reads, writes, is_dma):
        op = Op(eng, fn, is_dma)
        deps = {}
        for k in reads:
            w = self.last_w.get(k)
            if w is not None:
                deps[id(w)] = (w, "raw")
        for k in writes:
            w = self.last_w.get(k)
            if w is not None:
                deps[id(w)] = (w, "waw")
            for r in self.readers.get(k, ()):
                if id(r) not in deps:
                    deps[id(r)] = (r, "war")
        for d, kind in deps.values():
            if d.eng == eng and not d.is_dma and not is_dma:
                if eng == "pe":
                    continue
            d.signal = True
            op.deps.append(d)
        for k in reads:
            self.readers.setdefault(k, []).append(op)
        for k in writes:
            self.last_w[k] = op
            self.readers[k] = []
        if is_dma:
            k = len(self.dma_ops[eng])
            op.dma_k = k
            op.signal = True
            if k >= NRING:
                op.deps.append(self.dma_ops[eng][k - NRING])
            self.dma_ops[eng].append(op)
        else:
            self.last_real[eng] = op
        self.ops[eng].append(op)
        return op

    def pe(self, fn, reads=(), writes=()):
        return self._add("pe", fn, reads, writes, False)

    def act(self, fn, reads=(), writes=()):
        return self._add("act", fn, reads, writes, False)

    def dve(self, fn, reads=(), writes=()):
        return self._add("dve", fn, reads, writes, False)

    def dma(self, q, fn, reads=(), writes=()):
        return self._add(q, fn, reads, writes, True)

    def pool(self, fn, reads=(), writes=()):
        return self._add("pool", fn, reads, writes, False)

    def barrier(self):
        deps = []
        for e in COMPUTE:
            if self.last_real[e] is not None:
                self.last_real[e].signal = True
                deps.append(self.last_real[e])
        for q in QUEUES:
            deps.extend(self.dma_ops[q][max(self.bar_dma_start[q], len(self.dma_ops[q]) - NRING):])
            self.bar_dma_start[q] = len(self.dma_ops[q])
        for e in ("pe", "act", "dve", "sp", "pool"):
            op = Op(e, None, False)
            op.deps = list(deps)
            self.ops[e].append(op)
        self.last_w = {}
        self.readers = {}

    def emit(self, nc, es, final_waits=()):
        n_sems = len(COMPUTE) + NRING * len(QUEUES)
        sems = [es.enter_context(nc.semaphore(f"s{i}")) for i in range(n_sems)]
        esem = {e: sems[i] for i, e in enumerate(COMPUTE)}
        qsem = {q: sems[len(COMPUTE) + qi * NRING: len(COMPUTE) + (qi + 1) * NRING]
                for qi, q in enumerate(QUEUES)}
        for e in COMPUTE:
            c = 0
            for op in self.ops[e]:
                if op.signal and not op.is_dma:
                    c += 1
                    op.sem = esem[e]
                    op.value = c
        for q in QUEUES:
            for op in self.ops[q]:
                if op.is_dma:
                    op.sem = qsem[q][op.dma_k % NRING]
                    op.value = 16 * (op.dma_k // NRING + 1)
        block = es.enter_context(nc.Block())

        def run(engname):
            def body(eng):
                waited = {}
                for op in self.ops[engname]:
                    for d in op.deps:
                        key = id(d.sem)
                        if waited.get(key, 0) >= d.value:
                            continue
                        waited[key] = d.value
                        eng.wait_ge(d.sem, d.value)
                    if op.fn is None:
                        continue
                    ins = op.fn(eng)
                    if op.signal:
                        ins.then_inc(op.sem, 16 if op.is_dma else 1)
                if engname == "sp":
                    for d in final_waits:
                        eng.wait_ge(d.sem, d.value)
            return body

        block.tensor(run("pe"))
        block.scalar(run("act"))
        block.vector(run("dve"))
        block.gpsimd(run("pool"))
        block.sync(run("sp"))


class Arena:
    def __init__(self, ap_all, nelem):
        self.ap = ap_all
        self.n = nelem
        self.off = 0

    def mark(self):
        return self.off

    def reset(self, m):
        self.off = m

    def alloc(self, shape, dt, parts=128):
        per = int(np.prod(shape[1:]))
        ne = per * (2 if dt == F32 else 1)
        ne = (ne + 15) // 16 * 16
        assert self.off + ne <= self.n, ("SBUF arena overflow", self.off, ne, self.n)
        a = self.ap[0:shape[0], self.off:self.off + per * (2 if dt == F32 else 1)]
        self.off += ne
        if dt == F32:
            a = a.bitcast(F32)
        if len(shape) == 3:
            a = a.rearrange("p (a b) -> p a b", a=shape[1])
        elif len(shape) == 4:
            a = a.rearrange("p (a b c) -> p a b c", a=shape[1], b=shape[2])
        return a


class Ring:
    def __init__(self, name, items):
        self.name = name
        self.items = items
        self.i = 0

    def next(self):
        j = self.i % len(self.items)
        self.i += 1
        return self.items[j], (self.name, j)


def _rope_tables(dr, nrep):
    q4 = dr // 4
    rows = S // 64
    row = np.repeat(np.arange(rows, dtype=np.float32), 64)
    col = np.tile(np.arange(64, dtype=np.float32), rows)
    inv = (np.float32(10000.0) ** (-np.arange(q4, dtype=np.float32) / np.float32(q4))).astype(np.float32)
    cos = np.zeros((dr, S), np.float32)
    sin = np.zeros((dr, S), np.float32)
    for d in range(dr):
        a = d // (2 * q4)
        b = (d % (2 * q4)) // q4
        m = d % q4
        ang = ((row if a == 0 else col) * inv[m]).astype(np.float32)
        cos[d] = np.cos(ang)
        sin[d] = np.sin(ang) * (-1.0 if b == 0 else 1.0)
    return np.stack([np.tile(cos, (nrep, 1)), np.tile(sin, (nrep, 1))]).astype(np.float32)


def _perm(q4):
    P = np.zeros((128, 128), np.float32)
    for m in range(128):
        k = m + q4 if (m % (2 * q4)) < q4 else m - q4
        P[k, m] = 1.0
    return P


def _consts():
    p = np.arange(128, dtype=np.float32)
    jj = p[:, None]
    ii = p[None, :]
    ctab = np.stack([
        np.broadcast_to(ii + 1.0, (128, 128)),
        np.broadcast_to(128.0 - ii, (128, 128)),
        np.maximum(ii - jj, 0.0),
        (ii >= jj).astype(np.float32),
        np.maximum(jj - ii, 0.0),
        (jj >= ii).astype(np.float32),
    ], axis=1).astype(np.float32)
    pcol = np.stack([127.0 - p, p], axis=1).astype(np.float32)
    return {
        "ident": np.eye(128, dtype=np.float32),
        "ctab": np.ascontiguousarray(ctab),
        "pcol": np.ascontiguousarray(pcol),
        "ropeR": _rope_tables(128, 1),
        "ropeM": _rope_tables(64, 2),
        "perm": np.stack([_perm(32), _perm(16)]).astype(np.float32),
    }


def build(dbg=False, upto="Z"):
    nc = bass.Bass("TRN2", target_bir_lowering=False)
    R = Rec()

    def din(name, shape, dt=F32):
        return nc.dram_tensor(name, list(shape), dt, kind="ExternalInput").ap()

    def dscr(name, shape, dt):
        return nc.dram_tensor(name, list(shape), dt, kind="ExternalOutput" if dbg else "Internal").ap()

    x = din("x", [S, D]); ctx = din("ctx", [LC, D]); cc = din("cc", [128, 16, 2])
    w_ada = din("w_ada", [D, 6 * D]); b_ada = din("b_ada", [6 * D])
    w_in = din("w_in", [D, IN_COLS])
    decay = din("decay", [16])
    ret_gn = din("ret_gn", [D])
    w_ret_o = din("w_ret_o", [D, D])
    q_norm = din("mla_q_norm", [512]); kv_norm = din("mla_kv_norm", [512])
    w_q_up = din("w_q_up", [512, 3072]); w_kv_up = din("w_kv_up", [512, 4096])
    w_mla_o = din("w_mla_o", [D, D]); w_out = din("w_out", [D, D])
    w_up = din("ffn_w_up", [D, 2 * FFN]); conv_w = din("conv_w", [128, 3, NFC]); conv_b = din("conv_b", [128, NFC])
    w_down = din("ffn_w_down", [FFN, D]); fnorm = din("final_norm", [D])
    ident_d = din("ident", [128, 128]); ctab_d = din("ctab", [128, 6, 128]); pcol_d = din("pcol", [128, 2])
    ropeR_d = din("ropeR", [2, 128, S]); ropeM_d = din("ropeM", [2, 128, S]); perm_d = din("perm", [2, 128, 128])
    out = nc.dram_tensor("out", [S, D], F32, kind="ExternalOutput").ap()

    modd = dscr("modd", [2, 6 * D], F32)
    kT_d = dscr("kT_d", [8, 128, T], BF16); qT_d = dscr("qT_d", [8, 128, S], BF16)
    v_d = dscr("v_d", [T, D], BF16); rgs_d = dscr("rgs_d", [S, D], BF16)
    gr_d = dscr("gr_d", [16, 128, S], BF16); gm_d = dscr("gm_d", [16, 128, S], BF16)
    kvnT_d = dscr("kvnT_d", [4, 128, T], BF16); qnT_d = dscr("qnT_d", [4, 128, S], BF16)
    krT_d = dscr("krT_d", [64, T], BF16)
    ogT_d = dscr("ogT_d", [16, 128, S], BF16); omT_d = dscr("omT_d", [16, 128, S], BF16)
    mixT_d = dscr("mixT_d", [16, 128, S], BF16)
    x1_d = dscr("x1_d", [S, D], F32); x2_d = dscr("x2_d", [S, D], F32)

    es = contextlib.ExitStack()
    NAR = 100000
    arena_t = es.enter_context(nc.sbuf_tensor("arena", [128, NAR], BF16))
    AR = Arena(arena_t, NAR)
    ps = [es.enter_context(nc.psum_tensor(f"ps{i}", [128, 512], F32)) for i in range(8)]
    psb = [p[:].bitcast(BF16) for p in ps]

    def PK(i):
        return ("ps", i)

    def wload(ring, src, nk, ncols):
        buf, key = ring.next()
        dst = buf[:, 0:nk * ncols].rearrange("p (k n) -> p k n", k=nk)
        srcv = src.rearrange("(k p) n -> p k n", p=128)
        nsplit = 4
        bounds = [(i * nk) // nsplit for i in range(nsplit + 1)]
        keys = []
        for i in range(nsplit):
            a, b = bounds[i], bounds[i + 1]
            pk = (key, i)
            R.dma("pool", lambda e, a=a, b=b: e.dma_start(out=dst[:, a:b, :], in_=srcv[:, a:b, :]), writes=[pk])
            keys += [pk] * (b - a)
        return dst, keys, buf

    def mm(o, lhsT, rhs, start, stop, reads, wkey):
        R.pe(lambda e: e.matmul(o, lhsT=lhsT, rhs=rhs, start=start, stop=stop), reads=reads, writes=[wkey])

    def tr(o, in_, ident, reads, wkey):
        R.pe(lambda e: e.transpose(out=o, in_=in_, identity=ident), reads=reads, writes=[wkey])

    def act(o, in_, func, reads, writes, **kw):
        R.act(lambda e: e.activation(out=o, in_=in_, func=func, **kw), reads=reads, writes=writes)

    def rstd_from_ssq(ssq_col, tmp_col, rstd_col, n, key_ssq, key_rstd):
        act(tmp_col, ssq_col, AF.Sqrt, [key_ssq], [key_rstd], scale=1.0 / n, bias=EPS)
        R.dve(lambda e: e.reciprocal(out=rstd_col, in_=tmp_col), reads=[key_rstd], writes=[key_rstd])

    identf = AR.alloc([128, 128], F32)
    identb = AR.alloc([128, 128], BF16)
    eps_t = AR.alloc([128, 2], F32)
    R.dma("sp", lambda e: e.dma_start(out=identf, in_=ident_d), writes=["identf"])
    R.act(lambda e: e.copy(out=identb, in_=identf), reads=["identf"], writes=["identb"])
    R.dve(lambda e: e.memset(eps_t[:, 0:1], EPS), writes=["eps"])
    R.dve(lambda e: e.memset(eps_t[:, 1:2], 1.0), writes=["eps"])
    base_mark = AR.mark()
    final_ops = []

    def bcast_load(dst, src_row, key):
        R.dma("sp", lambda e: e.dma_start(out=dst, in_=src_row.partition_broadcast(128)), writes=[key])

    cct = AR.alloc([128, 16, 2], F32)
    sb16 = AR.alloc([128, 16, 2], BF16)
    bst = [AR.alloc([2, 512], F32) for _ in range(2)]
    mst = [AR.alloc([2, 512], F32) for _ in range(2)]
    bring = Ring("bst", bst)
    mring = Ring("mst", mst)
    hT = AR.alloc([128, 16, T], BF16)
    after_hT = AR.mark()
    wbufA = [AR.alloc([128, 8192], BF16) for _ in range(2)]
    wringA = Ring("wA", wbufA)
    R.dma("sp", lambda e: e.dma_start(out=cct, in_=cc), writes=["cct"])
    act(sb16, cct, AF.Silu, ["cct"], ["sb16"])
    psr0 = Ring("ps", [0, 1])

    def mod_group(n, wring_, psr_):
        wt, wkey, _ = wload(wring_, w_ada[:, n * 512:(n + 1) * 512], 16, 512)
        bt, bkey = bring.next()
        R.dma("sp", lambda e: e.dma_start(out=bt, in_=b_ada[n * 512:(n + 1) * 512].partition_broadcast(2)), writes=[bkey])
        bi, _ = psr_.next()
        for kc in range(16):
            mm(ps[bi][0:2, :], sb16[:, kc, :], wt[:, kc, :], kc == 0, kc == 15, ["sb16", wkey[kc]], PK(bi))
        mt, mkey = mring.next()
        R.dve(lambda e: e.tensor_tensor(out=mt, in0=ps[bi][0:2, :], in1=bt, op=ALU.add), reads=[bkey], writes=[mkey, PK(bi)])
        R.dma("sp", lambda e: e.dma_start(out=modd[:, n * 512:(n + 1) * 512], in_=mt), reads=[mkey], writes=[("modd", n)])

    for n in range(8):
        mod_group(n, wringA, psr0)

    xt = [AR.alloc([128, D], F32) for _ in range(2)]
    xring = Ring("xt", xt)
    tmpA = AR.alloc([128, D], F32)
    junkA = AR.alloc([128, D], BF16)
    hb = [AR.alloc([128, D], BF16) for _ in range(2)]
    hbring = Ring("hb", hb)
    scp = AR.alloc([128, D], F32)
    shb = AR.alloc([128, D], F32)
    ssqA = AR.alloc([128, NT], F32)
    rsA = AR.alloc([128, NT], F32)
    rstdA = AR.alloc([128, NT], F32)
    psrA = Ring("ps", [2, 3, 4, 5])
    bufsA = (xring, junkA, tmpA, hbring, psrA)

    def norm_mod_T(src_rows, dstT, ncols_tok, tile_idx, scp_t, shb_t, keys_mod, ssq, rs, rstd, dst_key, bufs=None):
        xring, junkA, tmpA, hbring, psrA = bufs if bufs is not None else bufsA
        xtt, xkey = xring.next()
        R.dma("sp", lambda e: e.dma_start(out=xtt, in_=src_rows), writes=[xkey])
        i = tile_idx
        act(junkA, xtt, AF.Square, [xkey], ["junkA", ("ssq", i)], accum_out=ssq[:, i:i + 1])
        act(rs[:, i:i + 1], ssq[:, i:i + 1], AF.Sqrt, [("ssq", i)], [("rstd", i)], scale=1.0 / D, bias=EPS)
        yield
        R.dve(lambda e: e.reciprocal(out=rstd[:, i:i + 1], in_=rs[:, i:i + 1]), reads=[("rstd", i)], writes=[("rstd", i)])
        R.dve(lambda e: e.scalar_tensor_tensor(out=tmpA, in0=xtt, scalar=rstd[:, i:i + 1], in1=scp_t, op0=ALU.mult, op1=ALU.mult),
              reads=[xkey, ("rstd", i)] + keys_mod, writes=["tmpA"])
        hbt, hkey = hbring.next()
        HS = D // 2
        R.dve(lambda e: e.tensor_tensor(out=hbt[:, 0:HS], in0=tmpA[:, 0:HS], in1=shb_t[:, 0:HS], op=ALU.add),
              reads=["tmpA"] + keys_mod, writes=[(hkey, 0)])
        R.dve(lambda e: e.tensor_tensor(out=hbt[:, HS:D], in0=tmpA[:, HS:D], in1=shb_t[:, HS:D], op=ALU.add),
              reads=["tmpA"] + keys_mod, writes=[(hkey, 1)])
        for half in range(2):
            bi, _ = psrA.next()
            for k8 in range(8):
                kc = half * 8 + k8
                tr(psb[bi][:, k8 * 128:(k8 + 1) * 128], hbt[:, kc * 128:(kc + 1) * 128], identb, [(hkey, half), "identb"], PK(bi))
            R.act(lambda e, bi=bi, half=half: e.copy(out=dstT[:, half * 8:half * 8 + 8, ncols_tok:ncols_tok + 128],
                                                     in_=psb[bi].rearrange("p (k n) -> p k n", k=8)),
                  reads=[], writes=[PK(bi), dst_key])

    def run_tiles(jobs):
        gens = [None] * len(jobs)

        def start(k):
            gens[k] = jobs[k][0]()
            next(gens[k])
        for k in range(min(2, len(jobs))):
            start(k)
        for k in range(len(jobs)):
            if jobs[k][1] is not None:
                jobs[k][1]()
            for _ in gens[k]:
                pass
            if k + 2 < len(jobs):
                start(k + 2)

    def modrow(r, j):
        return modd[r, j * D:(j + 1) * D]

    def modkeys(j):
        return [("modd", n) for n in range(4 * j, 4 * j + 4)]

    def mod_ctx():
        R.dma("sp", lambda e: e.dma_start(out=scp, in_=modrow(1, 1).partition_broadcast(128)), reads=modkeys(1), writes=["scp"])
        R.dma("sp", lambda e: e.dma_start(out=shb, in_=modrow(1, 0).partition_broadcast(128)), reads=modkeys(0), writes=["shb"])
        R.dve(lambda e: e.tensor_scalar_add(out=scp, in0=scp, scalar1=1.0), reads=["scp"], writes=["scp"])

    def mod_lat():
        R.dma("sp", lambda e: e.dma_start(out=scp, in_=modrow(0, 1).partition_broadcast(128)), reads=modkeys(1), writes=["scp"])
        R.dma("sp", lambda e: e.dma_start(out=shb, in_=modrow(0, 0).partition_broadcast(128)), reads=modkeys(0), writes=["shb"])
        R.dve(lambda e: e.tensor_scalar_add(out=scp, in0=scp, scalar1=1.0), reads=["scp"], writes=["scp"])

    jobsA = []
    for i in range(NT):
        src = ctx[i * 128:(i + 1) * 128, :] if i < 2 else x[(i - 2) * 128:(i - 1) * 128, :]
        jobsA.append((lambda i=i, src=src: norm_mod_T(src, hT, i * 128, i, scp, shb, ["scp", "shb"], ssqA, rsA, rstdA, ("hT", i)),
                      mod_ctx if i == 0 else (mod_lat if i == 2 else None)))
    run_tiles(jobsA)

    if dbg:
        hT_o = nc.dram_tensor("hT_o", [16, 128, T], BF16, kind="ExternalOutput").ap()
        final_ops.append(R.dma("sp", lambda e: e.dma_start(out=hT_o.rearrange("k p n -> p k n"), in_=hT),
                               reads=[("hT", i) for i in range(NT)]))

    def finish():
        R.barrier()
        if dbg:
            pass
        R.emit(nc, es, final_waits=final_ops + R.dma_ops["sp"][-NRING:] + R.dma_ops["pool"][-NRING:])
        es.close()
        return nc

    if upto == "A":
        return finish()

    R.barrier()
    AR.reset(after_hT)
    wbufB = [AR.alloc([128, 8192], BF16) for _ in range(3)]
    wring = Ring("wB", wbufB)
    permb = AR.alloc([128, 2, 128], BF16)
    R.dma("pool", lambda e: e.dma_start(out=permb, in_=perm_d.rearrange("t p n -> p t n")), writes=["permb"])
    rawb_ring = Ring("rawb", [AR.alloc([128, 512], BF16) for _ in range(3)])
    cosT = AR.alloc([128, S], F32)
    sinT = AR.alloc([128, S], F32)
    t1b = AR.alloc([128, 512], F32)
    t2b = AR.alloc([128, 512], F32)
    stk_ring = Ring("stk", [AR.alloc([128, T], BF16) for _ in range(2)])
    stv_ring = Ring("stv", [AR.alloc([128, 4, 512], BF16) for _ in range(2)])
    nrmb = AR.alloc([128, 512], F32)
    gnb = AR.alloc([128, 512], F32)
    tmprg_ring = Ring("tmprg", [AR.alloc([128, 512], F32) for _ in range(1)])
    ktm_ring = Ring("ktm", [AR.alloc([128, 512], BF16) for _ in range(2)])
    kst_ring = Ring("kst", [AR.alloc([128, 4, 128], BF16) for _ in range(2)])
    junkB = AR.alloc([128, 512], BF16)
    ssqB = AR.alloc([128, 2 * NT], F32)
    rsB = AR.alloc([128, 2 * NT], F32)
    rstdB = AR.alloc([128, 2 * NT], F32)
    psr = Ring("ps", list(range(8)))
    TG = [(0, 256)] + [(256 + 512 * g, 512) for g in range(4)]
    mod_next = [8]

    def more_mod(k=1):
        for _ in range(k):
            if mod_next[0] < 24:
                mod_group(mod_next[0], wring, psr)
                mod_next[0] += 1

    def hkeys(t0, n):
        return [("hT", i) for i in range(t0 // 128, (t0 + n) // 128)]

    def swap_copy(dst_flat, src_flat, n, q4, rkey, wkey):
        sv = src_flat[:, 0:n].rearrange("p (g b m) -> p g b m", b=2, m=q4)
        dv = dst_flat[:, 0:n].rearrange("p (g b m) -> p g b m", b=2, m=q4)
        R.dve(lambda e: e.tensor_copy(out=dv[:, :, 0, :], in_=sv[:, :, 1, :]), reads=[rkey], writes=[wkey])
        R.dve(lambda e: e.tensor_copy(out=dv[:, :, 1, :], in_=sv[:, :, 0, :]), reads=[rkey], writes=[wkey])

    def load_rope(tab_d, np_):
        R.dma("sp", lambda e: e.dma_start(out=cosT[0:np_, :], in_=tab_d[0, 0:np_, :]), writes=["rope"])
        R.dma("sp", lambda e: e.dma_start(out=sinT[0:np_, :], in_=tab_d[1, 0:np_, :]), writes=["rope"])

    def rope_group(wt_l, pidx, wkey, np_, scale, with_ctx, stage, skey):
        groups = [(t0, n) for (t0, n) in TG if not (t0 == 0 and not with_ctx)]
        st = {}

        def ga(k):
            t0, n = groups[k]
            bi, _ = psr.next()
            for kc in range(16):
                mm(ps[bi][0:np_, 0:n], wt_l[kc], hT[:, kc, t0:t0 + n], kc == 0, kc == 15, [wkey[kc]] + hkeys(t0, n), PK(bi))
            if t0 == 0:
                st[k] = (bi, None, None)
                return
            rawb, rkey = rawb_ring.next()
            act(rawb[0:np_, 0:n], ps[bi][0:np_, 0:n], AF.Copy, [], [PK(bi), rkey])
            st[k] = (bi, rawb, rkey)

        def gb(k):
            t0, n = groups[k]
            off = t0 if with_ctx else t0 - LC
            bi, rawb, rkey = st[k]
            if t0 == 0:
                act(stage[0:np_, off:off + n], ps[bi][0:np_, 0:n], AF.Copy, [], [PK(bi), skey], scale=scale)
                return
            bj, _ = psr.next()
            mm(ps[bj][0:np_, 0:n], permb[0:np_, pidx, 0:np_], rawb[0:np_, 0:n], True, True, ["permb", rkey], PK(bj))
            lt = t0 - LC
            R.dve(lambda e: e.scalar_tensor_tensor(out=t1b[0:np_, 0:n], in0=ps[bi][0:np_, 0:n], scalar=scale,
                                                   in1=cosT[0:np_, lt:lt + n], op0=ALU.mult, op1=ALU.mult),
                  reads=["rope"], writes=[PK(bi), "t1b"])
            R.dve(lambda e: e.scalar_tensor_tensor(out=t2b[0:np_, 0:n], in0=ps[bj][0:np_, 0:n], scalar=scale,
                                                   in1=sinT[0:np_, lt:lt + n], op0=ALU.mult, op1=ALU.mult),
                  reads=["rope"], writes=[PK(bj), "t2b"])
            R.dve(lambda e: e.tensor_tensor(out=stage[0:np_, off:off + n], in0=t1b[0:np_, 0:n], in1=t2b[0:np_, 0:n], op=ALU.add),
                  reads=["t1b", "t2b"], writes=[skey])

        ga(0)
        for k in range(len(groups)):
            if k + 1 < len(groups):
                ga(k + 1)
            gb(k)

    load_rope(ropeM_d, 64)
    wt64, wkey64, wflat64 = wload(wring, w_in[:, C_KR:C_KR + 64], 16, 64)
    stage, skey = stk_ring.next()
    rope_group([wt64[:, kc, :] for kc in range(16)], 1, wkey64, 64, 1.0, True, stage, skey)
    R.dma("sp", lambda e: e.dma_start(out=krT_d, in_=stage[0:64, :]), reads=[skey], writes=["krT_d"])

    load_rope(ropeR_d, 128)

    def proj_rope_fm(col0, ncg, scale, with_ctx, dst_d, dname):
        ntok = T if with_ctx else S
        for cg in range(ncg):
            wt, wkey, wflat = wload(wring, w_in[:, col0 + cg * 512:col0 + (cg + 1) * 512], 16, 512)
            for hh in range(4):
                head = cg * 4 + hh
                stage, skey = stk_ring.next()
                rope_group([wt[:, kc, hh * 128:(hh + 1) * 128] for kc in range(16)], 0, wkey, 128, scale, with_ctx, stage, skey)
                R.dma("sp", lambda e, head=head, stage=stage: e.dma_start(out=dst_d[head], in_=stage[:, 0:ntok]),
                      reads=[skey], writes=[(dname, head)])
            more_mod()

    proj_rope_fm(C_RK, 2, 128.0 ** -0.5, True, kT_d, "kT_d")
    proj_rope_fm(C_RQ, 2, 1.0, False, qT_d, "qT_d")

    def proj_tm(col0, ncg, tiles, evac, dst_d, row0, dname, pre=None):
        for cg in range(ncg):
            wt, wkey, _ = wload(wring, w_in[:, col0 + cg * 512:col0 + (cg + 1) * 512], 16, 512)
            if pre is not None:
                pre(cg)
            stv = svkey = first_t = None
            for idx, t in enumerate(tiles):
                j = idx % 4
                if j == 0:
                    stv, svkey = stv_ring.next()
                    first_t = t
                bi, _ = psr.next()
                for kc in range(16):
                    mm(ps[bi][:, :], hT[:, kc, t * 128:(t + 1) * 128], wt[:, kc, :], kc == 0, kc == 15, [wkey[kc], ("hT", t)], PK(bi))
                evac(stv[:, j, :], bi, svkey, cg)
                if j == 3 or idx == len(tiles) - 1:
                    nn = j + 1
                    r0 = (first_t - row0) * 128
                    R.dma("sp", lambda e, r0=r0, nn=nn, cg=cg, stv=stv: e.dma_start(
                        out=dst_d[r0:r0 + nn * 128, cg * 512:(cg + 1) * 512].rearrange("(t p) c -> p t c", p=128), in_=stv[:, 0:nn, :]),
                        reads=[svkey], writes=[(dname, cg, first_t)])
            more_mod()

    def evac_copy(dst, bi, svkey, cg):
        act(dst, ps[bi][:, :], AF.Copy, [], [PK(bi), svkey])

    proj_tm(C_RV, 4, list(range(NT)), evac_copy, v_d, 0, "v_d")

    def pre_gn(cg):
        bcast_load(gnb, ret_gn[cg * 512:(cg + 1) * 512], "gnb")

    def evac_rg(dst, bi, svkey, cg):
        tmp, tkey = tmprg_ring.next()
        act(tmp, ps[bi][:, :], AF.Silu, [], [PK(bi), tkey])
        R.dve(lambda e: e.tensor_tensor(out=dst, in0=tmp, in1=gnb, op=ALU.mult), reads=[tkey, "gnb"], writes=[svkey])

    proj_tm(C_RG, 4, list(range(2, NT)), evac_rg, rgs_d, 2, "rgs_d", pre=pre_gn)

    def proj_norm_T(col0, tiles, norm_row, dstT_d, tok_tile0, dname, sbase):
        wt, wkey, _ = wload(wring, w_in[:, col0:col0 + 512], 16, 512)
        bcast_load(nrmb, norm_row, "nrmb")
        banks = {}

        def pn_a(idx):
            t = tiles[idx]
            bi, _ = psr.next()
            banks[idx] = bi
            for kc in range(16):
                mm(ps[bi][:, :], hT[:, kc, t * 128:(t + 1) * 128], wt[:, kc, :], kc == 0, kc == 15, [wkey[kc], ("hT", t)], PK(bi))

        def pn_b(idx):
            t = tiles[idx]
            si = sbase + idx
            bi = banks[idx]
            act(junkB, ps[bi][:, :], AF.Square, [], [PK(bi), "junkB", ("ssqB", si)], accum_out=ssqB[:, si:si + 1])
            rstd_from_ssq(ssqB[:, si:si + 1], rsB[:, si:si + 1], rstdB[:, si:si + 1], 512, ("ssqB", si), ("rstdB", si))
            ktm, kkey = ktm_ring.next()
            R.dve(lambda e: e.scalar_tensor_tensor(out=ktm, in0=ps[bi][:, :], scalar=rstdB[:, si:si + 1], in1=nrmb,
                                                   op0=ALU.mult, op1=ALU.mult),
                  reads=[("rstdB", si), "nrmb"], writes=[PK(bi), kkey])
            bj, _ = psr.next()
            for k4 in range(4):
                tr(psb[bj][:, k4 * 128:(k4 + 1) * 128], ktm[:, k4 * 128:(k4 + 1) * 128], identb, [kkey, "identb"], PK(bj))
            kst, kskey = kst_ring.next()
            R.act(lambda e: e.copy(out=kst, in_=psb[bj][:, 0:512].rearrange("p (k n) -> p k n", k=4)),
                  reads=[], writes=[PK(bj), kskey])
            c0 = (t - tok_tile0) * 128
            R.dma("sp", lambda e: e.dma_start(out=dstT_d[:, :, c0:c0 + 128].rearrange("k p n -> p k n"), in_=kst),
                  reads=[kskey], writes=[(dname, t)])

        pn_a(0)
        for idx in range(len(tiles)):
            if idx + 1 < len(tiles):
                pn_a(idx + 1)
            pn_b(idx)
        more_mod()

    proj_norm_T(C_KVD, list(range(NT)), kv_norm, kvnT_d, 0, "kvnT_d", 0)
    proj_norm_T(C_QD, list(range(2, NT)), q_norm, qnT_d, 2, "qnT_d", NT)

    for (col0, dst_d, dname) in ((C_GR, gr_d, "gr_d"), (C_GM, gm_d, "gm_d")):
        for cg in range(4):
            wt, wkey, _ = wload(wring, w_in[:, col0 + cg * 512:col0 + (cg + 1) * 512], 16, 512)
            for c4 in range(4):
                stage, skey = stk_ring.next()
                for g in range(4):
                    t0 = LC + 512 * g
                    bi, _ = psr.next()
                    for kc in range(16):
                        mm(ps[bi][:, :], wt[:, kc, c4 * 128:(c4 + 1) * 128], hT[:, kc, t0:t0 + 512], kc == 0, kc == 15,
                           [wkey[kc]] + hkeys(t0, 512), PK(bi))
                    act(stage[:, g * 512:(g + 1) * 512], ps[bi][:, :], AF.Sigmoid, [], [PK(bi), skey])
                R.dma("sp", lambda e, stage=stage, ci=cg * 4 + c4, dst_d=dst_d: e.dma_start(out=dst_d[ci], in_=stage[:, 0:S]),
                      reads=[skey], writes=[(dname, cg * 4 + c4)])
            more_mod()
    more_mod(24)

    if upto == "B":
        return finish()

    R.barrier()
    AR.reset(base_mark)
    ctab = AR.alloc([128, 6, 128], F32)
    pcol = AR.alloc([128, 2], F32)
    dec2 = AR.alloc([128, 16], F32)
    e1 = AR.alloc([128, 16], F32)
    l1 = AR.alloc([128, 16], F32)
    lg = AR.alloc([128, 16], F32)
    kdecf = AR.alloc([128, 8], F32)
    kdecb = AR.alloc([128, 8], F32)
    gfb = AR.alloc([128, 16], F32)
    qdecf = AR.alloc([128, 8, 128], F32)
    qdecb = AR.alloc([128, 8, 128], F32)
    maskT = AR.alloc([128, 8, 128], F32)
    ef = AR.alloc([128, 128], F32)
    eb = AR.alloc([128, 128], F32)
    m1 = AR.alloc([128, 128], F32)
    m2 = AR.alloc([128, 128], F32)
    R.dma("sp", lambda e: e.dma_start(out=ctab, in_=ctab_d), writes=["ctab"])
    R.dma("sp", lambda e: e.dma_start(out=pcol, in_=pcol_d), writes=["pcol"])
    R.dma("sp", lambda e: e.dma_start(out=dec2, in_=decay.partition_broadcast(128)), writes=["dec2"])
    act(e1, dec2, AF.Exp, ["dec2"], ["e1"], scale=-1.0)
    act(l1, e1, AF.Ln, ["e1"], ["l1"], bias=eps_t[:, 1:2])
    R.dve(lambda e: e.tensor_scalar_mul(out=lg, in0=l1, scalar1=-1.0), reads=["l1"], writes=["lg"])
    act(kdecf, lg[:, 0:8], AF.Exp, ["lg", "pcol"], ["kdec"], scale=pcol[:, 0:1])
    act(kdecb, lg[:, 8:16], AF.Exp, ["lg", "pcol"], ["kdec"], scale=pcol[:, 1:2])
    act(gfb, lg, AF.Exp, ["lg"], ["gfb"], scale=128.0)
    for h in range(8):
        act(qdecf[:, h, :], ctab[:, 0, :], AF.Exp, ["lg", "ctab"], ["qdec"], scale=lg[:, h:h + 1])
        act(qdecb[:, h, :], ctab[:, 1, :], AF.Exp, ["lg", "ctab"], ["qdec"], scale=lg[:, 8 + h:9 + h])
        act(ef, ctab[:, 2, :], AF.Exp, ["lg", "ctab"], ["ef"], scale=lg[:, h:h + 1])
        act(eb, ctab[:, 4, :], AF.Exp, ["lg", "ctab"], ["eb"], scale=lg[:, 8 + h:9 + h])
        R.dve(lambda e: e.tensor_tensor(out=m1, in0=ef, in1=ctab[:, 3, :], op=ALU.mult), reads=["ef", "ctab"], writes=["m1"])
        R.dve(lambda e: e.tensor_tensor(out=m2, in0=eb, in1=ctab[:, 5, :], op=ALU.mult), reads=["eb", "ctab"], writes=["m2"])
        R.dve(lambda e, h=h: e.tensor_tensor(out=maskT[:, h, :], in0=m1, in1=m2, op=ALU.add), reads=["m1", "m2"], writes=["maskT"])

    KTr = Ring("KT", [AR.alloc([128, T], BF16) for _ in range(3)])
    QTr = Ring("QT", [AR.alloc([128, S], BF16) for _ in range(3)])
    Vr = Ring("V", [AR.alloc([128, NT, 256], BF16) for _ in range(3)])
    RGr = Ring("RG", [AR.alloc([128, 16, 256], BF16) for _ in range(3)])
    QfTr = Ring("QfT", [AR.alloc([128, S], BF16) for _ in range(2)])
    QbTr = Ring("QbT", [AR.alloc([128, S], BF16) for _ in range(2)])
    Kf = AR.alloc([128, NT, 128], BF16)
    Kb = AR.alloc([128, NT, 128], BF16)
    Rfbr = Ring("Rfb", [AR.alloc([128, 16, 256], BF16) for _ in range(2)])
    Rbbr = Ring("Rbb", [AR.alloc([128, 16, 256], BF16) for _ in range(2)])
    Rf = [AR.alloc([128, 256], F32) for _ in range(2)]
    Rb = [AR.alloc([128, 256], F32) for _ in range(2)]
    Pmr = Ring("Pm", [AR.alloc([128, 16, 128], BF16) for _ in range(2)])
    ogs_ring = Ring("ogs", [AR.alloc([128, 4, 256], BF16) for _ in range(2)])
    ogT_ring = Ring("ogTs", [AR.alloc([128, 2, S], BF16) for _ in range(2)])
    junkC = AR.alloc([128, 256], BF16)
    ssqC = AR.alloc([128, 128], F32)
    rsC = AR.alloc([128, 128], F32)
    rstdC = AR.alloc([128, 128], F32)
    psK = Ring("ps", [0, 1])
    psU = Ring("ps", [2, 3])
    psO = Ring("ps", [4, 5])
    psT = Ring("ps", [6, 7])
    v_v = v_d.rearrange("(t p) c -> p t c", p=128)
    rg_v = rgs_d.rearrange("(t p) c -> p t c", p=128)
    CH = {}

    LD = {}

    def loads(h):
        KT, kkey = KTr.next()
        QT, qkey = QTr.next()
        V, vkey = Vr.next()
        RG, rgkey = RGr.next()
        R.dma("sp", lambda e: e.dma_start(out=KT, in_=kT_d[h]), writes=[kkey])
        R.dma("sp", lambda e: e.dma_start(out=QT, in_=qT_d[h]), writes=[qkey])
        R.dma("sp", lambda e: e.dma_start(out=V, in_=v_v[:, :, h * 256:(h + 1) * 256]), writes=[vkey])
        R.dma("sp", lambda e: e.dma_start(out=RG, in_=rg_v[:, :, h * 256:(h + 1) * 256]), writes=[rgkey])
        LD[h] = (KT, kkey, QT, qkey, V, vkey, RG, rgkey)

    def stage1(h):
        KT, kkey, QT, qkey, V, vkey, RG, rgkey = LD[h]
        QfT, qfkey = QfTr.next()
        QbT, qbkey = QbTr.next()
        Rfb, rfbkey = Rfbr.next()
        Rbb, rbbkey = Rbbr.next()
        Pm, pmkey = Pmr.next()
        CH[h] = dict(V=V, vkey=vkey, RG=RG, rgkey=rgkey, QfT=QfT, qfkey=qfkey, QbT=QbT, qbkey=qbkey, Rfb=Rfb, rfbkey=rfbkey,
                     Rbb=Rbb, rbbkey=rbbkey, Pm=Pm, pmkey=pmkey)
        QTv = QT.rearrange("p (c n) -> p c n", c=16)
        qpieces = []
        for (dst, tab, dkey) in ((QfT, qdecf, qfkey), (QbT, qdecb, qbkey)):
            dstv = dst.rearrange("p (c n) -> p c n", c=16)
            for q_ in range(4):
                qpieces.append(lambda dstv=dstv, tab=tab, dkey=dkey, q_=q_: R.dve(
                    lambda e: e.tensor_tensor(out=dstv[:, q_ * 4:(q_ + 1) * 4, :], in0=QTv[:, q_ * 4:(q_ + 1) * 4, :],
                                              in1=tab[:, h, :].unsqueeze(1).broadcast_to([128, 4, 128]), op=ALU.mult),
                    reads=[qkey, "qdec"], writes=[dkey]))
        yield
        for (a_, b_) in ((0, 8), (8, 16), (16, 18)):
            bi, _ = psK.next()
            for i, cch in enumerate(range(a_, b_)):
                tr(psb[bi][:, i * 128:(i + 1) * 128], KT[:, cch * 128:(cch + 1) * 128], identb, [kkey, "identb"], PK(bi))
            n_ = b_ - a_
            src = psb[bi][:, 0:n_ * 128].rearrange("p (c d) -> p c d", c=n_)
            act(Kf[:, a_:b_, :], src, AF.Identity, ["kdec"], [PK(bi), "Kf"], scale=kdecf[:, h:h + 1])
            act(Kb[:, a_:b_, :], src, AF.Identity, ["kdec"], [PK(bi), "Kb"], scale=kdecb[:, h:h + 1])
            yield

        def U(Kd, kdkey, idx):
            bi, _ = psU.next()
            mm(ps[bi][:, 0:256], Kd[:, idx, :], V[:, idx, :], True, True, [kdkey, vkey], PK(bi))
            return bi

        def step(Rl, rname, p, bi, gcol, first):
            if first:
                R.dve(lambda e: e.tensor_copy(out=Rl[0], in_=ps[bi][:, 0:256]), reads=[], writes=[PK(bi), (rname, 0)])
                return 0
            q = 1 - p
            R.dve(lambda e: e.scalar_tensor_tensor(out=Rl[q], in0=Rl[p], scalar=gfb[:, gcol:gcol + 1], in1=ps[bi][:, 0:256],
                                                   op0=ALU.mult, op1=ALU.add),
                  reads=[(rname, p), "gfb"], writes=[PK(bi), (rname, q)])
            return q

        ford = [0, 1] + [2 + n for n in range(15)]
        bord = [1, 0] + [2 + n for n in range(15, 0, -1)]
        pf = pb_ = 0
        pend = []
        for i in range(17):
            for fn_ in pend:
                fn_()
            pend = []
            if qpieces:
                qpieces.pop(0)()
            bi = U(Kf, "Kf", ford[i])
            pf = step(Rf, "Rf", pf, bi, h, i == 0)
            if i >= 1:
                n = i - 1
                pend.append(lambda pf=pf, n=n: R.act(lambda e: e.copy(out=Rfb[:, n, :], in_=Rf[pf]), reads=[("Rf", pf)], writes=[(rfbkey, n)]))
            bi = U(Kb, "Kb", bord[i])
            pb_ = step(Rb, "Rb", pb_, bi, 8 + h, i == 0)
            if i >= 1:
                n = 16 - i
                pend.append(lambda pb_=pb_, n=n: R.act(lambda e: e.copy(out=Rbb[:, n, :], in_=Rb[pb_]), reads=[("Rb", pb_)], writes=[(rbbkey, n)]))
            if i % 4 == 3 and i < 16:
                n0 = (i // 4) * 4
                bs, _ = psK.next()
                for c in range(4):
                    n = n0 + c
                    mm(ps[bs][:, c * 128:(c + 1) * 128], KT[:, (2 + n) * 128:(3 + n) * 128], QT[:, n * 128:(n + 1) * 128], True, True,
                       [kkey, qkey], PK(bs))
                R.dve(lambda e, bs=bs, n0=n0: e.tensor_tensor(out=Pm[:, n0:n0 + 4, :], in0=ps[bs].rearrange("p (c n) -> p c n", c=4),
                                                              in1=maskT[:, h, :].unsqueeze(1).broadcast_to([128, 4, 128]), op=ALU.mult),
                      reads=["maskT"], writes=[PK(bs), (pmkey, n0)])
            yield
        for fn_ in pend:
            fn_()

    def stage2(h):
        d = CH[h]
        V, vkey, RG, rgkey = d["V"], d["vkey"], d["RG"], d["rgkey"]
        QfT, QbT, Rfb, Rbb, Pm = d["QfT"], d["QbT"], d["Rfb"], d["Rbb"], d["Pm"]
        ogT_s, ogTkey = ogT_ring.next()
        ogs = ogskey = None
        pend2 = None
        for n in range(16):
            bo, _ = psO.next()
            o_ap = ps[bo][:, 0:256]
            mm(o_ap, Pm[:, n, :], V[:, 2 + n, :], True, False, [(d["pmkey"], n // 4 * 4), vkey], PK(bo))
            mm(o_ap, QfT[:, n * 128:(n + 1) * 128], Rfb[:, n, :], False, False, [d["qfkey"], (d["rfbkey"], n)], PK(bo))
            mm(o_ap, QbT[:, n * 128:(n + 1) * 128], Rbb[:, n, :], False, True, [d["qbkey"], (d["rbbkey"], n)], PK(bo))
            si = h * 16 + n
            act(junkC, o_ap, AF.Square, [], [PK(bo), "junkC", ("ssqC", si)], accum_out=ssqC[:, si:si + 1])
            rstd_from_ssq(ssqC[:, si:si + 1], rsC[:, si:si + 1], rstdC[:, si:si + 1], 256, ("ssqC", si), ("rstdC", si))
            if n % 4 == 0:
                ogs, ogskey = ogs_ring.next()
            R.dve(lambda e, o_ap=o_ap, si=si, ogs=ogs, n=n: e.scalar_tensor_tensor(
                out=ogs[:, n % 4, :], in0=o_ap, scalar=rstdC[:, si:si + 1], in1=RG[:, n, :], op0=ALU.mult, op1=ALU.mult),
                reads=[("rstdC", si), rgkey], writes=[PK(bo), ogskey])
            if pend2 is not None:
                pend2()
                pend2 = None
            if n % 4 == 3:
                def _tr(n0=n - 3, ogs=ogs, ogskey=ogskey):
                    bt, _ = psT.next()
                    for ec in range(2):
                        for c in range(4):
                            tr(psb[bt][:, (ec * 4 + c) * 128:(ec * 4 + c + 1) * 128], ogs[:, c, ec * 128:(ec + 1) * 128], identb,
                               [ogskey, "identb"], PK(bt))
                    R.act(lambda e: e.copy(out=ogT_s[:, :, n0 * 128:(n0 + 4) * 128],
                                           in_=psb[bt].rearrange("p (a n) -> p a n", a=2)),
                          reads=[], writes=[PK(bt), ogTkey])
                pend2 = _tr
            yield
        if pend2 is not None:
            pend2()
        R.dma("sp", lambda e: e.dma_start(out=ogT_d[2 * h:2 * h + 2].rearrange("k p n -> p k n"), in_=ogT_s),
              reads=[ogTkey], writes=[("ogT_d", h)])

    loads(0)
    loads(1)
    for _ in stage1(0):
        pass
    for h in range(8):
        if h + 2 < 8:
            loads(h + 2)
        g2 = stage2(h)
        g1 = stage1(h + 1) if h + 1 < 8 else iter(())
        done1 = done2 = False
        while not (done1 and done2):
            if not done1:
                try:
                    next(g1)
                except StopIteration:
                    done1 = True
            if not done2:
                try:
                    next(g2)
                except StopIteration:
                    done2 = True

    if upto == "C":
        return finish()

    R.barrier()
    AR.reset(base_mark)
    kvnT = AR.alloc([128, 4, T], BF16)
    qnT = AR.alloc([128, 4, S], BF16)
    krT2 = AR.alloc([128, T], BF16)
    cosM = AR.alloc([128, S], F32)
    sinM = AR.alloc([128, S], F32)
    kvn_v = kvnT_d.rearrange("k p n -> p k n")
    qn_v = qnT_d.rearrange("k p n -> p k n")
    for i_, (t0_, n_) in enumerate(TG):
        R.dma("sp", lambda e, t0_=t0_, n_=n_: e.dma_start(out=kvnT[:, :, t0_:t0_ + n_], in_=kvn_v[:, :, t0_:t0_ + n_]), writes=[("kvnT", i_)])
    for g_ in range(4):
        R.dma("sp", lambda e, g_=g_: e.dma_start(out=qnT[:, :, g_ * 512:(g_ + 1) * 512], in_=qn_v[:, :, g_ * 512:(g_ + 1) * 512]), writes=[("qnT", g_)])

    def kvn_piece(tile):
        return 0 if tile < 2 else 1 + (tile - 2) // 4
    R.dma("sp", lambda e: e.dma_start(out=krT2[0:64, :], in_=krT_d), writes=["krT"])
    R.dma("sp", lambda e: e.dma_start(out=krT2[64:128, :], in_=krT_d), writes=["krT"])
    R.dma("sp", lambda e: e.dma_start(out=cosM, in_=ropeM_d[0]), writes=["ropeM"])
    R.dma("sp", lambda e: e.dma_start(out=sinM, in_=ropeM_d[1]), writes=["ropeM"])
    PT_ring = Ring("PT", [AR.alloc([128, NT, 512], BF16) for _ in range(3)])
    wkv_r = Ring("wkv", [AR.alloc([128, 4, 512], BF16) for _ in range(1)])
    wq_r = Ring("wq", [AR.alloc([128, 4, 384], BF16) for _ in range(1)])
    wqr_r = Ring("wqr", [AR.alloc([128, 512], BF16) for _ in range(1)])
    wqrsw_r = Ring("wqrsw", [AR.alloc([128, 512], BF16) for _ in range(1)])
    KnT_r = Ring("KnT", [AR.alloc([128, T], BF16) for _ in range(3)])
    Vp_r = Ring("Vp", [AR.alloc([128, NT, 129], BF16) for _ in range(3)])
    QnT_r = Ring("QnT", [AR.alloc([128, S], BF16) for _ in range(3)])
    QrA_r = Ring("QrA", [AR.alloc([128, S], BF16) for _ in range(2)])
    QrB_r = Ring("QrB", [AR.alloc([128, S], BF16) for _ in range(2)])
    On_r = Ring("On", [AR.alloc([128, 16, 128], BF16) for _ in range(2)])
    omT_r = Ring("omTs", [AR.alloc([128, S], BF16) for _ in range(2)])
    t1d = AR.alloc([128, 512], F32)
    t2d = AR.alloc([128, 512], F32)
    rden = AR.alloc([128, 32], F32)
    for _k, vp in enumerate(Vp_r.items):
        R.dve(lambda e, vp=vp: e.memset(vp[:, :, 128:129], 1.0), writes=[("Vp", _k)])
    for _k in range(2):
        R.dve(lambda e, _k=_k: e.memset(QrA_r.items[_k][64:128, :], 0.0), writes=[("QrA", _k)])
        R.dve(lambda e, _k=_k: e.memset(QrB_r.items[_k][0:64, :], 0.0), writes=[("QrB", _k)])
    psP = Ring("ps", [0, 1])
    psS = Ring("ps", [2, 3, 4])
    psO = Ring("ps", [5, 6])
    psT = Ring("ps", [7])
    att_scale = 192.0 ** -0.5
    HD = {}

    def proj_pair(p):
        h0 = 2 * p
        wkv, wkvkey = wkv_r.next()
        wq, wqkey = wq_r.next()
        wqr, wqrkey = wqr_r.next()
        wqrsw, wqrswkey = wqrsw_r.next()
        R.dma("pool", lambda e: e.dma_start(out=wkv, in_=w_kv_up[:, h0 * 256:(h0 + 2) * 256].rearrange("(k p) n -> p k n", p=128)), writes=[wkvkey])
        R.dma("pool", lambda e: e.dma_start(out=wq, in_=w_q_up[:, h0 * 192:(h0 + 2) * 192].rearrange("(k p) n -> p k n", p=128)), writes=[wqkey])
        wqr3 = wqr.rearrange("p (k n) -> p k n", k=4)
        R.dve(lambda e: e.tensor_copy(out=wqr3[:, :, 0:64], in_=wq[:, :, 128:192]), reads=[wqkey], writes=[wqrkey])
        R.dve(lambda e: e.tensor_copy(out=wqr3[:, :, 64:128], in_=wq[:, :, 320:384]), reads=[wqkey], writes=[wqrkey])
        sv = wqr.rearrange("p (g b m) -> p g b m", b=2, m=16)
        dv = wqrsw.rearrange("p (g b m) -> p g b m", b=2, m=16)
        R.dve(lambda e: e.tensor_copy(out=dv[:, :, 0, :], in_=sv[:, :, 1, :]), reads=[wqrkey], writes=[wqrswkey])
        R.dve(lambda e: e.tensor_copy(out=dv[:, :, 1, :], in_=sv[:, :, 0, :]), reads=[wqrkey], writes=[wqrswkey])
        wqrsw3 = wqrsw.rearrange("p (k n) -> p k n", k=4)
        QrA, qrakey = QrA_r.next()
        QrB, qrbkey = QrB_r.next()
        for hh in range(2):
            h = h0 + hh
            KnT, knkey = KnT_r.next()
            Vp, vpkey = Vp_r.next()
            QnT, qnkey = QnT_r.next()
            On, onkey = On_r.next()
            omT_s, omkey = omT_r.next()
            HD[h] = dict(KnT=KnT, knkey=knkey, Vp=Vp, vpkey=vpkey, QnT=QnT, qnkey=qnkey, On=On, onkey=onkey, omT_s=omT_s, omkey=omkey,
                         Qr=(QrA if hh == 0 else QrB), qrkey=(qrakey if hh == 0 else qrbkey), PTs={})
            kc0 = hh * 256
            for ti_, (t0, n) in enumerate(TG):
                bi, _ = psP.next()
                for kc in range(4):
                    mm(ps[bi][:, 0:n], wkv[:, kc, kc0:kc0 + 128], kvnT[:, kc, t0:t0 + n], kc == 0, kc == 3, [wkvkey, ("kvnT", ti_)], PK(bi))
                R.dve(lambda e, bi=bi, t0=t0, n=n, KnT=KnT: e.tensor_copy(out=KnT[:, t0:t0 + n], in_=ps[bi][:, 0:n]), reads=[], writes=[PK(bi), knkey])
            for t0 in range(0, NT, 4):
                ncn = min(4, NT - t0)
                bi, _ = psP.next()
                for c in range(ncn):
                    t = t0 + c
                    for kc in range(4):
                        mm(ps[bi][:, c * 128:(c + 1) * 128], kvnT[:, kc, t * 128:(t + 1) * 128], wkv[:, kc, kc0 + 128:kc0 + 256], kc == 0, kc == 3,
                           [wkvkey, ("kvnT", kvn_piece(t))], PK(bi))
                R.dve(lambda e, bi=bi, t0=t0, ncn=ncn, Vp=Vp: e.tensor_copy(out=Vp[:, t0:t0 + ncn, 0:128],
                                                                            in_=ps[bi][:, 0:ncn * 128].rearrange("p (c d) -> p c d", c=ncn)),
                      reads=[], writes=[PK(bi), vpkey])
            qc0 = hh * 192
            for g in range(4):
                bi, _ = psP.next()
                for kc in range(4):
                    mm(ps[bi][:, :], wq[:, kc, qc0:qc0 + 128], qnT[:, kc, g * 512:(g + 1) * 512], kc == 0, kc == 3, [wqkey, ("qnT", g)], PK(bi))
                R.dve(lambda e, bi=bi, g=g, QnT=QnT: e.tensor_copy(out=QnT[:, g * 512:(g + 1) * 512], in_=ps[bi][:, :]), reads=[], writes=[PK(bi), qnkey])
        for g in range(4):
            bi, _ = psP.next()
            for kc in range(4):
                mm(ps[bi][:, :], wqr3[:, kc, :], qnT[:, kc, g * 512:(g + 1) * 512], kc == 0, kc == 3, [wqrkey, ("qnT", g)], PK(bi))
            bj, _ = psP.next()
            for kc in range(4):
                mm(ps[bj][:, :], wqrsw3[:, kc, :], qnT[:, kc, g * 512:(g + 1) * 512], kc == 0, kc == 3, [wqrswkey, ("qnT", g)], PK(bj))
            R.dve(lambda e, bi=bi, g=g: e.tensor_tensor(out=t1d, in0=ps[bi][:, :], in1=cosM[:, g * 512:(g + 1) * 512], op=ALU.mult),
                  reads=["ropeM"], writes=[PK(bi), "t1d"])
            R.dve(lambda e, bj=bj, g=g: e.tensor_tensor(out=t2d, in0=ps[bj][:, :], in1=sinM[:, g * 512:(g + 1) * 512], op=ALU.mult),
                  reads=["ropeM"], writes=[PK(bj), "t2d"])
            R.dve(lambda e, g=g: e.tensor_tensor(out=QrA[0:64, g * 512:(g + 1) * 512], in0=t1d[0:64, :], in1=t2d[0:64, :], op=ALU.add),
                  reads=["t1d", "t2d"], writes=[qrakey])
            R.dve(lambda e, g=g: e.tensor_tensor(out=QrB[64:128, g * 512:(g + 1) * 512], in0=t1d[64:128, :], in1=t2d[64:128, :], op=ALU.add),
                  reads=["t1d", "t2d"], writes=[qrbkey])

    def scores(h, g):
        d = HD[h]
        PTb, ptkey = PT_ring.next()
        d["PTs"][g] = (PTb, ptkey)
        for j in range(NT):
            bi, _ = psS.next()
            mm(ps[bi][:, :], d["KnT"][:, j * 128:(j + 1) * 128], d["QnT"][:, g * 512:(g + 1) * 512], True, False, [d["knkey"], d["qnkey"]], PK(bi))
            mm(ps[bi][:, :], krT2[:, j * 128:(j + 1) * 128], d["Qr"][:, g * 512:(g + 1) * 512], False, True, ["krT", d["qrkey"]], PK(bi))
            act(PTb[:, j, :], ps[bi][:, :], AF.Exp, [], [PK(bi), (ptkey, j)], scale=att_scale)

    def pv(h, g):
        d = HD[h]
        PTb, ptkey = d["PTs"][g]
        On = d["On"]
        for qs in range(4):
            qb = g * 4 + qs
            bo, _ = psO.next()
            for j in range(NT):
                mm(ps[bo][:, 0:129], PTb[:, j, qs * 128:(qs + 1) * 128], d["Vp"][:, j, :], j == 0, j == NT - 1, [(ptkey, j), d["vpkey"]], PK(bo))
            rc = (h % 2) * 16 + qb
            R.dve(lambda e, bo=bo, rc=rc: e.reciprocal(out=rden[:, rc:rc + 1], in_=ps[bo][:, 128:129]), reads=[], writes=[PK(bo), ("rden", rc)])
            R.dve(lambda e, bo=bo, rc=rc, qb=qb, On=On: e.tensor_scalar(out=On[:, qb, :], in0=ps[bo][:, 0:128], scalar1=rden[:, rc:rc + 1],
                                                                   scalar2=None, op0=ALU.mult),
                  reads=[("rden", rc)], writes=[PK(bo), d["onkey"]])

    def finish_head(h):
        d = HD[h]
        for qb0 in (0, 8):
            bt, _ = psT.next()
            for c in range(8):
                tr(psb[bt][:, c * 128:(c + 1) * 128], d["On"][:, qb0 + c, :], identb, [d["onkey"], "identb"], PK(bt))
            R.act(lambda e, bt=bt, qb0=qb0, omT_s=d["omT_s"]: e.copy(out=omT_s[:, qb0 * 128:(qb0 + 8) * 128], in_=psb[bt][:, :]),
                  reads=[], writes=[PK(bt), d["omkey"]])
        R.dma("sp", lambda e, h=h, omT_s=d["omT_s"]: e.dma_start(out=omT_d[h], in_=omT_s), reads=[d["omkey"]], writes=[("omT_d", h)])

    items = [(h, g) for h in range(16) for g in range(4)]
    proj_pair(0)
    scores(*items[0])
    for i, (h, g) in enumerate(items):
        if i + 1 < len(items):
            h2, g2 = items[i + 1]
            if g2 == 0 and h2 % 2 == 0:
                proj_pair(h2 // 2)
            scores(h2, g2)
        pv(h, g)
        if g == 3:
            finish_head(h)

    if upto == "D":
        return finish()

    R.barrier()
    AR.reset(base_mark)
    ogT = AR.alloc([128, 16, S], BF16)
    omT = AR.alloc([128, 16, S], BF16)
    for q4_ in range(4):
        R.dma("sp", lambda e, q4_=q4_: e.dma_start(out=ogT[:, :, q4_ * 512:(q4_ + 1) * 512],
                                                 in_=ogT_d[:, :, q4_ * 512:(q4_ + 1) * 512].rearrange("k p n -> p k n")),
              writes=[("ogT", q4_)])
        R.dma("sp", lambda e, q4_=q4_: e.dma_start(out=omT[:, :, q4_ * 512:(q4_ + 1) * 512],
                                                 in_=omT_d[:, :, q4_ * 512:(q4_ + 1) * 512].rearrange("k p n -> p k n")),
              writes=[("omT", q4_)])
    wr_ring = Ring("wr", [AR.alloc([128, 4096], BF16) for _ in range(2)])
    wm_ring = Ring("wm", [AR.alloc([128, 4096], BF16) for _ in range(2)])
    gr_ring = Ring("grt", [AR.alloc([128, S], BF16) for _ in range(2)])
    gm_ring = Ring("gmt", [AR.alloc([128, S], BF16) for _ in range(2)])
    t1e = AR.alloc([128, 512], F32)
    t2e = AR.alloc([128, 512], F32)
    mix_ring = Ring("mixs", [AR.alloc([128, S], BF16) for _ in range(2)])
    psr = Ring("ps", list(range(8)))
    gates = {}

    def load_gates(c):
        grt, grkey = gr_ring.next()
        gmt, gmkey = gm_ring.next()
        R.dma("sp", lambda e: e.dma_start(out=grt, in_=gr_d[c]), writes=[grkey])
        R.dma("sp", lambda e: e.dma_start(out=gmt, in_=gm_d[c]), writes=[gmkey])
        gates[c] = (grt, grkey, gmt, gmkey)

    load_gates(0)
    for cg in range(8):
        wr, wrkey, _ = wload(wr_ring, w_ret_o[:, cg * 256:(cg + 1) * 256], 16, 256)
        wm, wmkey, _ = wload(wm_ring, w_mla_o[:, cg * 256:(cg + 1) * 256], 16, 256)
        for c2 in range(2):
            c = cg * 2 + c2
            if c + 1 < 16:
                load_gates(c + 1)
            grt, grkey, gmt, gmkey = gates[c]
            mixs, mkey = mix_ring.next()
            for g in range(4):
                bi, _ = psr.next()
                for ec in range(16):
                    mm(ps[bi][:, :], wr[:, ec, c2 * 128:(c2 + 1) * 128], ogT[:, ec, g * 512:(g + 1) * 512], ec == 0, ec == 15,
                       [wrkey[ec], ("ogT", g)], PK(bi))
                bj, _ = psr.next()
                for ec in range(16):
                    mm(ps[bj][:, :], wm[:, ec, c2 * 128:(c2 + 1) * 128], omT[:, ec, g * 512:(g + 1) * 512], ec == 0, ec == 15,
                       [wmkey[ec], ("omT", g)], PK(bj))
                R.dve(lambda e, bi=bi, g=g, grt=grt: e.tensor_tensor(out=t1e, in0=ps[bi][:, :], in1=grt[:, g * 512:(g + 1) * 512], op=ALU.mult),
                      reads=[grkey], writes=[PK(bi), "t1e"])
                R.dve(lambda e, bj=bj, g=g, gmt=gmt: e.tensor_tensor(out=t2e, in0=ps[bj][:, :], in1=gmt[:, g * 512:(g + 1) * 512], op=ALU.mult),
                      reads=[gmkey], writes=[PK(bj), "t2e"])
                R.dve(lambda e, g=g, mixs=mixs: e.tensor_tensor(out=mixs[:, g * 512:(g + 1) * 512], in0=t1e, in1=t2e, op=ALU.add),
                      reads=["t1e", "t2e"], writes=[mkey])
            R.dma("sp", lambda e, mixs=mixs, c=c: e.dma_start(out=mixT_d[c], in_=mixs), reads=[mkey], writes=[("mixT_d", c)])

    if upto == "E":
        return finish()

    R.barrier()
    AR.reset(base_mark)
    mixT = AR.alloc([128, 16, S], BF16)
    for q4_ in range(4):
        R.dma("sp", lambda e, q4_=q4_: e.dma_start(out=mixT[:, :, q4_ * 512:(q4_ + 1) * 512],
                                                 in_=mixT_d[:, :, q4_ * 512:(q4_ + 1) * 512].rearrange("k p n -> p k n")),
              writes=[("mixT", q4_)])
    wo_ring = Ring("wo", [AR.alloc([128, 8192], BF16) for _ in range(2)])
    g1b = AR.alloc([128, D], F32)
    bcast_load(g1b, modrow(0, 2), "g1b")
    xr_ring = Ring("xr", [AR.alloc([128, 512], F32) for _ in range(3)])
    tF_ring = Ring("tF", [AR.alloc([128, 512], F32) for _ in range(2)])
    x1s_ring = Ring("x1s", [AR.alloc([128, 512], F32) for _ in range(3)])
    psr = Ring("ps", list(range(8)))
    xrs = {}

    def load_xr(k):
        cg_, t_ = divmod(k, 16)
        xr, xrkey = xr_ring.next()
        R.dma("sp", lambda e: e.dma_start(out=xr, in_=x[t_ * 128:(t_ + 1) * 128, cg_ * 512:(cg_ + 1) * 512]), writes=[xrkey])
        xrs[k] = (xr, xrkey)

    load_xr(0)
    load_xr(1)
    for cg in range(4):
        wo, wokey, _ = wload(wo_ring, w_out[:, cg * 512:(cg + 1) * 512], 16, 512)
        for t in range(16):
            k = cg * 16 + t
            if k + 2 < 64:
                load_xr(k + 2)
            xr, xrkey = xrs[k]
            bi, _ = psr.next()
            for c in range(16):
                mm(ps[bi][:, :], mixT[:, c, t * 128:(t + 1) * 128], wo[:, c, :], c == 0, c == 15, [wokey[c], ("mixT", t // 4)], PK(bi))
            tF, tFkey = tF_ring.next()
            x1s, x1key = x1s_ring.next()
            R.dve(lambda e, bi=bi, cg=cg, tF=tF: e.tensor_tensor(out=tF, in0=ps[bi][:, :], in1=g1b[:, cg * 512:(cg + 1) * 512], op=ALU.mult),
                  reads=["g1b"], writes=[PK(bi), tFkey])
            R.dve(lambda e, tF=tF, xr=xr, x1s=x1s: e.tensor_tensor(out=x1s, in0=tF, in1=xr, op=ALU.add), reads=[tFkey, xrkey], writes=[x1key])
            R.dma("sp", lambda e, x1s=x1s, t=t, cg=cg: e.dma_start(out=x1_d[t * 128:(t + 1) * 128, cg * 512:(cg + 1) * 512], in_=x1s),
                  reads=[x1key], writes=[("x1_d", t, cg)])

    if upto == "F":
        return finish()

    R.barrier()
    AR.reset(base_mark)
    ssq2 = AR.alloc([128, 128], F32)
    ssqT = AR.alloc([128, 16], F32)
    rsT = AR.alloc([128, 16], F32)
    rstdT = AR.alloc([128, 16], F32)
    convw = AR.alloc([128, 3, NFC], F32)
    convb = AR.alloc([128, NFC], F32)
    g2b = AR.alloc([128, D], F32)
    gT = AR.alloc([128, NFC, 1024], BF16)
    R.dma("sp", lambda e: e.dma_start(out=convw, in_=conv_w), writes=["convw"])
    R.dma("sp", lambda e: e.dma_start(out=convb, in_=conv_b), writes=["convb"])
    bcast_load(g2b, modrow(0, 5), "g2b")
    markG = AR.mark()
    for hf in range(2):
        R.barrier()
        AR.reset(markG)
        h2T = AR.alloc([128, 16, 1152], BF16)
        markG1 = AR.mark()
        sc2p = AR.alloc([128, D], F32)
        sh2b = AR.alloc([128, D], F32)
        xringG = Ring("xtG", [AR.alloc([128, D], F32) for _ in range(2)])
        tmpG = AR.alloc([128, D], F32)
        junkG = AR.alloc([128, D], BF16)
        hbringG = Ring("hbG", [AR.alloc([128, D], BF16) for _ in range(2)])
        ssqG = AR.alloc([128, 16], F32)
        rsG = AR.alloc([128, 16], F32)
        rstdG = AR.alloc([128, 16], F32)
        psrG = Ring("ps", [0, 1, 2, 3])
        bufsG = (xringG, junkG, tmpG, hbringG, psrG)
        bcast_load(sc2p, modrow(0, 4), "sc2p")
        bcast_load(sh2b, modrow(0, 3), "sh2b")
        R.dve(lambda e: e.tensor_scalar_add(out=sc2p, in0=sc2p, scalar1=1.0), reads=["sc2p"], writes=["sc2p"])
        tiles = [hf * 8 + i for i in range(8)] + [8 if hf == 0 else 7]
        run_tiles([(lambda i=i, tt=tt: norm_mod_T(x1_d[tt * 128:(tt + 1) * 128, :], h2T, i * 128, i, sc2p, sh2b, ["sc2p", "sh2b"],
                                                  ssqG, rsG, rstdG, ("h2T", i), bufs=bufsG), None) for i, tt in enumerate(tiles)])
        hcol = 1024 if hf == 0 else 1151

        R.barrier()
        AR.reset(markG1)
        wa_ring = Ring("wa", [AR.alloc([128, 4096], BF16) for _ in range(2)])
        wv_ring = Ring("wv", [AR.alloc([128, 4096], BF16) for _ in range(2)])
        asb_ring = Ring("asb", [AR.alloc([128, 1026], F32) for _ in range(2)])
        acc = AR.alloc([128, 1024], F32)
        for (ab, _k) in zip(asb_ring.items, range(2)):
            R.dve(lambda e, ab=ab: e.memset(ab[:, 0:1], 0.0), writes=[("asb", _k)])
            R.dve(lambda e, ab=ab: e.memset(ab[:, 1025:1026], 0.0), writes=[("asb", _k)])
        psr = Ring("ps", list(range(8)))
        h2keys = [("h2T", i) for i in range(9)]
        for f2 in range(NFC // 2):
            wa, wakey, _ = wload(wa_ring, w_up[:, f2 * 256:(f2 + 1) * 256], 16, 256)
            wv, wvkey, _ = wload(wv_ring, w_up[:, FFN + f2 * 256:FFN + (f2 + 1) * 256], 16, 256)
            for fi in range(2):
                fc = f2 * 2 + fi
                asb, asbkey = asb_ring.next()
                ba = []
                for tg in range(2):
                    bi, _ = psr.next()
                    ba.append(bi)
                    for kc in range(16):
                        mm(ps[bi][:, :], wa[:, kc, fi * 128:(fi + 1) * 128], h2T[:, kc, tg * 512:(tg + 1) * 512], kc == 0, kc == 15,
                           [wakey[kc]] + h2keys, PK(bi))
                bh, _ = psr.next()
                for kc in range(16):
                    mm(ps[bh][:, 0:1], wa[:, kc, fi * 128:(fi + 1) * 128], h2T[:, kc, hcol:hcol + 1], kc == 0, kc == 15, [wakey[kc]] + h2keys, PK(bh))
                bv = []
                for tg in range(2):
                    bj, _ = psr.next()
                    bv.append(bj)
                    for kc in range(16):
                        mm(ps[bj][:, :], wv[:, kc, fi * 128:(fi + 1) * 128], h2T[:, kc, tg * 512:(tg + 1) * 512], kc == 0, kc == 15,
                           [wvkey[kc]] + h2keys, PK(bj))
                for tg in range(2):
                    R.act(lambda e, bi=ba[tg], tg=tg, asb=asb: e.copy(out=asb[:, 1 + tg * 512:1 + (tg + 1) * 512], in_=ps[bi][:, :]),
                          reads=[], writes=[PK(ba[tg]), asbkey])
                hdst = 1025 if hf == 0 else 0
                R.act(lambda e, bh=bh, asb=asb, hdst=hdst: e.copy(out=asb[:, hdst:hdst + 1], in_=ps[bh][:, 0:1]), reads=[], writes=[PK(bh), asbkey])
                act(acc, asb[:, 1:1025], AF.Identity, [asbkey, "convw", "convb"], ["acc"], scale=convw[:, 1, fc:fc + 1], bias=convb[:, fc:fc + 1])
                R.dve(lambda e, asb=asb, fc=fc: e.scalar_tensor_tensor(out=acc, in0=asb[:, 0:1024], scalar=convw[:, 0, fc:fc + 1], in1=acc,
                                                                       op0=ALU.mult, op1=ALU.add), reads=[asbkey, "convw", "acc"], writes=["acc"])
                R.dve(lambda e, asb=asb, fc=fc: e.scalar_tensor_tensor(out=acc, in0=asb[:, 2:1026], scalar=convw[:, 2, fc:fc + 1], in1=acc,
                                                                       op0=ALU.mult, op1=ALU.add), reads=[asbkey, "convw", "acc"], writes=["acc"])
                act(acc, acc, AF.Silu, ["acc"], ["acc"])
                for tg in range(2):
                    R.dve(lambda e, bj=bv[tg], tg=tg, fc=fc: e.tensor_tensor(out=gT[:, fc, tg * 512:(tg + 1) * 512], in0=ps[bj][:, :],
                                                                            in1=acc[:, tg * 512:(tg + 1) * 512], op=ALU.mult),
                          reads=["acc"], writes=[PK(bv[tg]), ("gT", fc)])

        R.barrier()
        AR.reset(markG)
        wd_ring = Ring("wd", [AR.alloc([128, NFC * 256], BF16) for _ in range(2)])
        x1r_ring = Ring("x1r", [AR.alloc([128, 256], F32) for _ in range(3)])
        tG_ring = Ring("tG", [AR.alloc([128, 256], F32) for _ in range(2)])
        x2s_ring = Ring("x2s", [AR.alloc([128, 256], F32) for _ in range(3)])
        junk2 = AR.alloc([128, 256], BF16)
        psr = Ring("ps", list(range(8)))
        gkeys = [("gT", fc) for fc in range(NFC)]
        if hf == 1:
            fnbE = AR.alloc([128, D], F32)
            x2tE = AR.alloc([128, D], F32)
            outsE = AR.alloc([128, D], F32)
            bcast_load(fnbE, fnorm, "fnbE")
            R.dve(lambda e: e.tensor_reduce(out=ssqT[:, 0:8], in_=ssq2[:, 0:64].rearrange("p (t c) -> p t c", c=8),
                                            axis=mybir.AxisListType.X, op=ALU.add), reads=[("ssq2", t_) for t_ in range(8)], writes=["ssqT0"])
            act(rsT[:, 0:8], ssqT[:, 0:8], AF.Sqrt, ["ssqT0"], ["rsT0"], scale=1.0 / D, bias=eps_t[:, 0:1])
            R.dve(lambda e: e.reciprocal(out=rstdT[:, 0:8], in_=rsT[:, 0:8]), reads=["rsT0"], writes=["rstdT0"])

            def early_final(tt_):
                R.dma("sp", lambda e: e.dma_start(out=x2tE, in_=x2_d[tt_ * 128:(tt_ + 1) * 128, :]), writes=["x2tE"])
                R.dve(lambda e: e.scalar_tensor_tensor(out=outsE, in0=x2tE, scalar=rstdT[:, tt_:tt_ + 1], in1=fnbE, op0=ALU.mult, op1=ALU.mult),
                      reads=["x2tE", "rstdT0", "fnbE"], writes=["outsE"])
                final_ops.append(R.dma("sp", lambda e: e.dma_start(out=out[tt_ * 128:(tt_ + 1) * 128, :], in_=outsE), reads=["outsE"]))
        x1rs = {}

        def load_x1r(k):
            cg_, t_ = divmod(k, 8)
            tt_ = hf * 8 + t_
            x1r, x1rkey = x1r_ring.next()
            R.dma("sp", lambda e: e.dma_start(out=x1r, in_=x1_d[tt_ * 128:(tt_ + 1) * 128, cg_ * 256:(cg_ + 1) * 256]), writes=[x1rkey])
            x1rs[k] = (x1r, x1rkey)

        load_x1r(0)
        load_x1r(1)
        for cg in range(8):
            wd, wdkey, _ = wload(wd_ring, w_down[:, cg * 256:(cg + 1) * 256], NFC, 256)
            for t in range(8):
                tt = hf * 8 + t
                k = cg * 8 + t
                if k + 2 < 64:
                    load_x1r(k + 2)
                x1r, x1rkey = x1rs[k]
                bi, _ = psr.next()
                for fc in range(NFC):
                    mm(ps[bi][:, 0:256], gT[:, fc, t * 128:(t + 1) * 128], wd[:, fc, :], fc == 0, fc == NFC - 1,
                       [wdkey[fc]] + (gkeys if fc == 0 else []), PK(bi))
                tG, tGkey = tG_ring.next()
                x2s, x2key = x2s_ring.next()
                R.dve(lambda e, bi=bi, cg=cg, tG=tG: e.tensor_tensor(out=tG, in0=ps[bi][:, 0:256], in1=g2b[:, cg * 256:(cg + 1) * 256], op=ALU.mult),
                      reads=["g2b"], writes=[PK(bi), tGkey])
                R.dve(lambda e, tG=tG, x1r=x1r, x2s=x2s: e.tensor_tensor(out=x2s, in0=tG, in1=x1r, op=ALU.add), reads=[tGkey, x1rkey], writes=[x2key])
                si = tt * 8 + cg
                act(junk2, x2s, AF.Square, [x2key], ["junk2", ("ssq2", tt)], accum_out=ssq2[:, si:si + 1])
                R.dma("sp", lambda e, x2s=x2s, tt=tt, cg=cg: e.dma_start(out=x2_d[tt * 128:(tt + 1) * 128, cg * 256:(cg + 1) * 256], in_=x2s),
                      reads=[x2key], writes=[("x2_d", tt)])
                if hf == 1 and k % 8 == 5:
                    early_final(k // 8)

    if upto == "G":
        return finish()

    R.barrier()
    AR.reset(markG)
    fnb = AR.alloc([128, D], F32)
    bcast_load(fnb, fnorm, "fnb")
    x2t_ring = Ring("x2t", [AR.alloc([128, D], F32) for _ in range(3)])
    outs_ring = Ring("outs", [AR.alloc([128, D], F32) for _ in range(3)])
    R.dve(lambda e: e.tensor_reduce(out=ssqT[:, 8:16], in_=ssq2[:, 64:128].rearrange("p (t c) -> p t c", c=8), axis=mybir.AxisListType.X, op=ALU.add),
          reads=[], writes=["ssqT"])
    act(rsT[:, 8:16], ssqT[:, 8:16], AF.Sqrt, ["ssqT"], ["rsT"], scale=1.0 / D, bias=eps_t[:, 0:1])
    R.dve(lambda e: e.reciprocal(out=rstdT[:, 8:16], in_=rsT[:, 8:16]), reads=["rsT"], writes=["rstdT"])
    x2ts = {}

    def load_x2t(tt_):
        x2t, x2tkey = x2t_ring.next()
        R.dma("sp", lambda e: e.dma_start(out=x2t, in_=x2_d[tt_ * 128:(tt_ + 1) * 128, :]), writes=[x2tkey])
        x2ts[tt_] = (x2t, x2tkey)

    load_x2t(8)
    load_x2t(9)
    for tt in range(8, 16):
        if tt + 2 < 16:
            load_x2t(tt + 2)
        x2t, x2tkey = x2ts[tt]
        outs, okey = outs_ring.next()
        R.dve(lambda e, x2t=x2t, outs=outs, tt=tt: e.scalar_tensor_tensor(out=outs, in0=x2t, scalar=rstdT[:, tt:tt + 1], in1=fnb,
                                                                      op0=ALU.mult, op1=ALU.mult),
              reads=[x2tkey, "rstdT", "fnb"], writes=[okey])
        final_ops.append(R.dma("sp", lambda e, outs=outs, tt=tt: e.dma_start(out=out[tt * 128:(tt + 1) * 128, :], in_=outs), reads=[okey]))

    return finish()


_CACHE = {}


def make_in_maps(inputs):
    cst = _consts()
    f = lambda a: np.ascontiguousarray(np.asarray(a, dtype=np.float32))
    c_ctx = f(inputs["c_ctx"])
    shared = {
        "w_ada": f(inputs["w_ada"][0]), "b_ada": f(inputs["b_ada"][0]).reshape(-1),
        "w_in": f(inputs["w_in"][0]),
        "decay": np.concatenate([f(inputs["ret_decay_fwd"][0]), f(inputs["ret_decay_bwd"][0])]).reshape(16),
        "ret_gn": f(inputs["ret_gn"][0]).reshape(-1),
        "w_ret_o": f(inputs["w_ret_o"][0]),
        "mla_q_norm": f(inputs["mla_q_norm"][0]).reshape(-1), "mla_kv_norm": f(inputs["mla_kv_norm"][0]).reshape(-1),
        "w_q_up": f(inputs["w_q_up"][0]), "w_kv_up": f(inputs["w_kv_up"][0]),
        "w_mla_o": f(inputs["w_mla_o"][0]), "w_out": f(inputs["w_out"][0]),
        "ffn_w_up": f(inputs["ffn_w_up"][0]),
        "conv_w": np.ascontiguousarray(f(inputs["ffn_conv_w"][0]).reshape(3, NFC, 128).transpose(2, 0, 1)),
        "conv_b": np.ascontiguousarray(f(inputs["ffn_conv_b"][0]).reshape(NFC, 128).T),
        "ffn_w_down": f(inputs["ffn_w_down"][0]), "final_norm": f(inputs["final_norm"]).reshape(-1),
    }
    shared.update(cst)
    maps = []
    for b in range(8):
        m = dict(shared)
        m["x"] = f(inputs["x"][b])
        m["ctx"] = f(inputs["ctx"][b])
        ccb = np.stack([f(inputs["c"][b]).reshape(16, 128).T, c_ctx.reshape(16, 128).T], axis=2)
        m["cc"] = np.ascontiguousarray(ccb)
        maps.append(m)
    return maps


def kernel(**inputs):
    if "nc" not in _CACHE:
        _CACHE["nc"] = build()
    nc = _CACHE["nc"]
    maps = make_in_maps(inputs)
    res = run_bass_kernel_spmd(nc, maps, core_ids=list(range(8)))
    return np.stack([np.asarray(r["out"], dtype=np.float32) for r in res.results], axis=0)
```
